# Optimizing a Trainium2 kernel written in Bass

```python
import math
import jax, jax.numpy as jnp
from jax import lax
import numpy as np

D_MODEL = 1024
BATCH = 8
SEQ = 8192
DEPTH = 4
DEC_BATCH = 2
DEC_SEQ = 8192
PAST_LEN = 128

GRID_W = 64
NA_HEADS = 8
NA_HEAD_DIM = 64
NA_W = NA_HEADS * NA_HEAD_DIM
NA_KH_MAX = 8
NA_KW = 16
NA_COL_BLOCK = NA_KW
NA_N_CB = GRID_W // NA_COL_BLOCK
NA_BAND = 2 * NA_KW
LRU_W = 512
LRU_BLOCKS = 8
LRU_BW = LRU_W // LRU_BLOCKS
CONV_W = 4
LRU_C = 8.0
CA_HEADS = 4
CA_HEAD_DIM = 128
CA_W = CA_HEADS * CA_HEAD_DIM
N_MEM = 256
D_FF = 2816
N_BRANCH = 3
IN_COLS = 3 * NA_W + 2 * LRU_W + CA_W
EPS = 1e-6
NEG = -1e30

kernel_name = "hybrid_na_rglru_memory_encoder"


def rmsnorm(x, g):
    xf = x.astype(jnp.float32)
    y = xf * lax.rsqrt(jnp.mean(xf * xf, axis=-1, keepdims=True) + EPS)
    return (y * g.astype(jnp.float32)).astype(x.dtype)


def swiglu(h, w_up, w_down):
    a, b = jnp.split(h @ w_up, 2, axis=-1)
    return (jax.nn.silu(a) * b) @ w_down


def _na_col_tables():
    j = np.arange(NA_N_CB)[:, None, None]
    qc = j * NA_COL_BLOCK + np.arange(NA_COL_BLOCK)[None, :, None]
    bs = np.clip(j * NA_COL_BLOCK - NA_KW // 2, 0, GRID_W - NA_BAND)
    kc = bs + np.arange(NA_BAND)[None, None, :]
    ws = np.clip(qc - NA_KW // 2, 0, GRID_W - NA_KW)
    valid = (kc >= ws) & (kc < ws + NA_KW)
    dc = np.clip(kc - qc, -(NA_KW - 1), NA_KW - 1) + (NA_KW - 1)
    band_idx = bs[:, 0, 0][:, None] + np.arange(NA_BAND)[None, :]
    return band_idx, valid, dc


def neighbourhood_attention(q, k, v, rpb):
    B, T, _ = q.shape
    rows = T // GRID_W
    kh = min(NA_KH_MAX, rows)
    scale = NA_HEAD_DIM ** -0.5

    def grid(a):
        return a.reshape(B, rows, GRID_W, NA_HEADS, NA_HEAD_DIM).transpose(0, 3, 1, 2, 4)

    qg, kg, vg = grid(q), grid(k), grid(v)
    band_idx, valid, dc = _na_col_tables()
    mask = jnp.asarray(valid)[:, :, None, :]
    dc_b = jnp.asarray(dc, dtype=jnp.int32)[:, :, None, :]

    def row_step(r):
        rs = jnp.clip(r - kh // 2, 0, rows - kh)
        k_rows = lax.dynamic_slice_in_dim(kg, rs, kh, axis=2)
        v_rows = lax.dynamic_slice_in_dim(vg, rs, kh, axis=2)
        k_band = k_rows[:, :, :, band_idx, :]
        v_band = v_rows[:, :, :, band_idx, :]
        q_row = lax.dynamic_index_in_dim(qg, r, axis=2, keepdims=False)
        q_row = q_row.reshape(B, NA_HEADS, NA_N_CB, NA_COL_BLOCK, NA_HEAD_DIM)
        s = jnp.einsum('bhjqd,bhkjcd->bhjqkc', q_row, k_band).astype(jnp.float32) * scale
        dr = rs + jnp.arange(kh, dtype=jnp.int32) - r + (NA_KH_MAX - 1)
        bias = rpb[:, dr[None, None, :, None], dc_b]
        s = jnp.where(mask, s + bias.astype(jnp.float32), NEG)
        p = jax.nn.softmax(s.reshape(s.shape[:4] + (kh * NA_BAND,)), axis=-1).reshape(s.shape)
        o = jnp.einsum('bhjqkc,bhkjcd->bhjqd', p.astype(v.dtype), v_band)
        return o.reshape(B, NA_HEADS, GRID_W, NA_HEAD_DIM)

    out = lax.map(row_step, jnp.arange(rows, dtype=jnp.int32))
    return out.transpose(1, 0, 3, 2, 4).reshape(B, T, NA_W)


def centred_conv(x, w, b):
    T = x.shape[1]
    left = CONV_W // 2
    xp = jnp.pad(x, ((0, 0), (left, CONV_W - 1 - left), (0, 0)))
    y = sum(xp[:, i:i + T, :] * w[i] for i in range(CONV_W))
    return y + b


def rg_lru(x, wa, ba, wi, bi, lam, reverse):
    B, T, _ = x.shape
    xb = x.reshape(B, T, LRU_BLOCKS, LRU_BW)
    f32 = jnp.float32
    r = jax.nn.sigmoid(jnp.einsum('btnc,ncd->btnd', xb, wa.astype(f32)).reshape(B, T, LRU_W) + ba.astype(f32))
    i = jax.nn.sigmoid(jnp.einsum('btnc,ncd->btnd', xb, wi.astype(f32)).reshape(B, T, LRU_W) + bi.astype(f32))
    log_a = -LRU_C * r * jax.nn.softplus(-lam.astype(f32))
    a = jnp.exp(log_a)
    u = jnp.sqrt(-jnp.expm1(2.0 * log_a)) * (i * x)

    def combine(e1, e2):
        a1, b1 = e1
        a2, b2 = e2
        return a1 * a2, a2 * b1 + b2

    _, h = lax.associative_scan(combine, (a, u), axis=1, reverse=reverse)
    return h


def memory_attention(q, mem_n, w_kv):
    B, T, _ = q.shape
    M = mem_n.shape[1]
    k, v = jnp.split(mem_n @ w_kv, 2, axis=-1)
    qh = q.reshape(B, T, CA_HEADS, CA_HEAD_DIM)
    kh = k.reshape(B, M, CA_HEADS, CA_HEAD_DIM)
    vh = v.reshape(B, M, CA_HEADS, CA_HEAD_DIM)
    s = jnp.einsum('bthd,bmhd->bhtm', qh, kh).astype(jnp.float32) * (CA_HEAD_DIM ** -0.5)
    p = jax.nn.softmax(s, axis=-1).astype(q.dtype)
    return jnp.einsum('bhtm,bmhd->bthd', p, vh).reshape(B, T, CA_W)


def _layer(x, mem, p, l):
    h = rmsnorm(x, p['g_ffn1_pre'][l])
    x = x + 0.5 * rmsnorm(swiglu(h, p['w_ffn1_up'][l], p['w_ffn1_down'][l]), p['g_ffn1_post'][l])

    h = rmsnorm(x, p['g_mix_pre'][l])
    proj = h @ p['w_in'][l]
    splits = [NA_W, 2 * NA_W, 3 * NA_W, 3 * NA_W + LRU_W, 3 * NA_W + 2 * LRU_W]
    q_na, k_na, v_na, x_lru, g_lru, q_ca = jnp.split(proj, splits, axis=-1)

    y_na = neighbourhood_attention(q_na, k_na, v_na, p['na_rpb'][l])

    xc = centred_conv(x_lru, p['conv_w'][l], p['conv_b'][l]).astype(jnp.float32)
    h_f = rg_lru(xc, p['lru_wa'][l, 0], p['lru_ba'][l, 0], p['lru_wi'][l, 0], p['lru_bi'][l, 0],
                 p['lru_lambda'][l, 0], reverse=False)
    h_b = rg_lru(xc, p['lru_wa'][l, 1], p['lru_ba'][l, 1], p['lru_wi'][l, 1], p['lru_bi'][l, 1],
                 p['lru_lambda'][l, 1], reverse=True)
    y_lru = ((h_f + h_b) * jax.nn.gelu(g_lru.astype(jnp.float32))).astype(x.dtype)

    y_ca = memory_attention(q_ca, rmsnorm(mem, p['g_mem'][l]), p['w_mem_kv'][l])

    gates = jax.nn.sigmoid((h @ p['w_gate'][l] + p['b_gate'][l]).astype(jnp.float32)).astype(x.dtype)
    g_na, g_lr, g_ca = jnp.split(gates, N_BRANCH, axis=-1)
    merged = (g_na * (y_na @ p['w_branch_na'][l])
              + g_lr * (y_lru @ p['w_branch_lru'][l])
              + g_ca * (y_ca @ p['w_branch_ca'][l]))
    x = x + rmsnorm(merged @ p['w_out'][l], p['g_mix_post'][l])

    h = rmsnorm(x, p['g_ffn2_pre'][l])
    x = x + 0.5 * rmsnorm(swiglu(h, p['w_ffn2_up'][l], p['w_ffn2_down'][l]), p['g_ffn2_post'][l])
    return x


def _trunk(x, mem, p):
    for l in range(DEPTH):
        x = _layer(x, mem, p, l)
    return x


def setup_inputs(seed: int = 0) -> dict:
    key = jax.random.key(seed)
    ks = jax.random.split(key, 40)
    f32 = jnp.float32

    def nrm(k, shape, fan_in):
        return jax.random.normal(k, shape, f32) * (fan_in ** -0.5)

    def gain(k, shape):
        return 1.0 + 0.05 * jax.random.normal(k, shape, f32)

    a0 = jax.random.uniform(ks[20], (DEPTH, 2, LRU_W), f32, 0.9, 0.999)
    sig = a0 ** (1.0 / LRU_C)
    lam = jnp.log(sig) - jnp.log1p(-sig)

    return {
        'x_prompt': jax.random.normal(ks[0], (BATCH, SEQ, D_MODEL), f32),
        'x_sample': jax.random.normal(ks[1], (DEC_BATCH, DEC_SEQ, D_MODEL), f32),
        'mem_prompt': jax.random.normal(ks[2], (BATCH, N_MEM, D_MODEL), f32),
        'mem_sample': jax.random.normal(ks[3], (DEC_BATCH, N_MEM, D_MODEL), f32),
        'g_ffn1_pre': gain(ks[4], (DEPTH, D_MODEL)),
        'w_ffn1_up': nrm(ks[5], (DEPTH, D_MODEL, 2 * D_FF), D_MODEL),
        'w_ffn1_down': nrm(ks[6], (DEPTH, D_FF, D_MODEL), D_FF),
        'g_ffn1_post': gain(ks[7], (DEPTH, D_MODEL)),
        'g_mix_pre': gain(ks[8], (DEPTH, D_MODEL)),
        'w_in': nrm(ks[9], (DEPTH, D_MODEL, IN_COLS), D_MODEL),
        'na_rpb': 0.1 * jax.random.normal(ks[10], (DEPTH, NA_HEADS, 2 * NA_KH_MAX - 1, 2 * NA_KW - 1), f32),
        'conv_w': nrm(ks[11], (DEPTH, CONV_W, LRU_W), CONV_W),
        'conv_b': 0.01 * jax.random.normal(ks[12], (DEPTH, LRU_W), f32),
        'lru_wa': nrm(ks[13], (DEPTH, 2, LRU_BLOCKS, LRU_BW, LRU_BW), LRU_BW),
        'lru_ba': 0.1 * jax.random.normal(ks[14], (DEPTH, 2, LRU_W), f32),
        'lru_wi': nrm(ks[15], (DEPTH, 2, LRU_BLOCKS, LRU_BW, LRU_BW), LRU_BW),
        'lru_bi': 0.1 * jax.random.normal(ks[16], (DEPTH, 2, LRU_W), f32),
        'lru_lambda': lam,
        'g_mem': gain(ks[17], (DEPTH, D_MODEL)),
        'w_mem_kv': nrm(ks[18], (DEPTH, D_MODEL, 2 * CA_W), D_MODEL),
        'w_gate': nrm(ks[19], (DEPTH, D_MODEL, N_BRANCH * D_MODEL), D_MODEL),
        'b_gate': 0.1 * jax.random.normal(ks[21], (DEPTH, N_BRANCH * D_MODEL), f32),
        'w_branch_na': nrm(ks[22], (DEPTH, NA_W, D_MODEL), NA_W),
        'w_branch_lru': nrm(ks[23], (DEPTH, LRU_W, D_MODEL), LRU_W),
        'w_branch_ca': nrm(ks[24], (DEPTH, CA_W, D_MODEL), CA_W),
        'w_out': nrm(ks[25], (DEPTH, D_MODEL, D_MODEL), D_MODEL),
        'g_mix_post': gain(ks[26], (DEPTH, D_MODEL)),
        'g_ffn2_pre': gain(ks[27], (DEPTH, D_MODEL)),
        'w_ffn2_up': nrm(ks[28], (DEPTH, D_MODEL, 2 * D_FF), D_MODEL),
        'w_ffn2_down': nrm(ks[29], (DEPTH, D_FF, D_MODEL), D_FF),
        'g_ffn2_post': gain(ks[30], (DEPTH, D_MODEL)),
    }


def reference(x_prompt, x_sample, mem_prompt, mem_sample,
              g_ffn1_pre, w_ffn1_up, w_ffn1_down, g_ffn1_post,
              g_mix_pre, w_in, na_rpb, conv_w, conv_b,
              lru_wa, lru_ba, lru_wi, lru_bi, lru_lambda,
              g_mem, w_mem_kv, w_gate, b_gate,
              w_branch_na, w_branch_lru, w_branch_ca, w_out, g_mix_post,
              g_ffn2_pre, w_ffn2_up, w_ffn2_down, g_ffn2_post):
    p = dict(
        g_ffn1_pre=g_ffn1_pre, w_ffn1_up=w_ffn1_up, w_ffn1_down=w_ffn1_down, g_ffn1_post=g_ffn1_post,
        g_mix_pre=g_mix_pre, w_in=w_in, na_rpb=na_rpb, conv_w=conv_w, conv_b=conv_b,
        lru_wa=lru_wa, lru_ba=lru_ba, lru_wi=lru_wi, lru_bi=lru_bi, lru_lambda=lru_lambda,
        g_mem=g_mem, w_mem_kv=w_mem_kv, w_gate=w_gate, b_gate=b_gate,
        w_branch_na=w_branch_na, w_branch_lru=w_branch_lru, w_branch_ca=w_branch_ca,
        w_out=w_out, g_mix_post=g_mix_post,
        g_ffn2_pre=g_ffn2_pre, w_ffn2_up=w_ffn2_up, w_ffn2_down=w_ffn2_down, g_ffn2_post=g_ffn2_post,
    )
    y_prompt = _trunk(x_prompt, mem_prompt, p)
    y_sample = _trunk(x_sample, mem_sample, p)
    return (y_prompt, y_sample)
```

```python
import numpy as np
import concourse.bass as bass
import concourse.mybir as mybir
from concourse.bass_utils import run_bass_kernel_spmd

F32 = mybir.dt.float32
BF16 = mybir.dt.bfloat16
AF = mybir.ActivationFunctionType
ALU = mybir.AluOpType

D = 1024
KC = 8
FF = 2816
FC = 22
NTOK = 256
H_NA = 8
NMEM = 256
EPS = 1e-6
NEG = -1e30
NVEC = 124
V_G1PRE, V_G1POST, V_GMPRE, V_GMPOST, V_G2PRE, V_G2POST = 0, 8, 16, 24, 32, 40
V_BG = 48
V_CW = 72
V_CB = 88
V_BA = 92
V_BI = 100
V_LAM = 108
V_GMEM = 116


class Eng:
    def __init__(self, nc, name, eng, counts=True):
        self.nc, self.name, self.eng = nc, name, eng
        self.sem = nc.alloc_semaphore(name="s_" + name) if counts else None
        self.cnt = 0
        self.seen = {}

    def wait(self, tok):
        sem, val, key, ename = tok
        if ename == self.name and self.name == "pe":
            return
        if self.seen.get(key, 0) >= val:
            return
        self.seen[key] = val
        self.eng.wait_ge(sem, val)


class Buf:
    __slots__ = ("w", "r", "dsem", "name")

    def __init__(self, name=""):
        self.w = {}
        self.r = {}
        self.dsem = None
        self.name = name


class K:
    def __init__(self, nc):
        self.nc = nc
        self.pe = Eng(nc, "pe", nc.tensor)
        self.act = Eng(nc, "act", nc.scalar)
        self.dve = Eng(nc, "dve", nc.vector)
        self.pool = Eng(nc, "pool", nc.gpsimd)
        self.sp = Eng(nc, "sp", nc.sync, counts=False)
        self.engs = [self.pe, self.act, self.dve, self.pool, self.sp]
        self.dpool = []
        self.dused = []
        self.nd = 0
        self.phase_bufs = []

    def buf(self, name=""):
        b = Buf(name)
        self.phase_bufs.append(b)
        return b

    def bufs(self, n, name=""):
        return [self.buf(name + str(i)) for i in range(n)]

    def _deps(self, E, reads, writes):
        for b in reads:
            for key, tok in b.w.items():
                E.wait(tok)
        for b in writes:
            for key, tok in b.r.items():
                if tok[3] != E.name:
                    E.wait(tok)
            for key, tok in b.w.items():
                if tok[3] != E.name:
                    E.wait(tok)

    def _mark(self, tok, reads, writes):
        for b in reads:
            b.r[tok[2]] = tok
        for b in writes:
            b.w = {tok[2]: tok}
            b.r = {}

    def op(self, E, fn, reads=(), writes=()):
        self._deps(E, reads, writes)
        ins = fn(E.eng)
        E.cnt += 1
        ins.then_inc(E.sem, 1)
        tok = (E.sem, E.cnt, E.name, E.name)
        self._mark(tok, reads, writes)
        return tok

    def _dsem(self, b):
        if b.dsem is None:
            if self.dpool:
                b.dsem = self.dpool.pop()
            else:
                self.nd += 1
                b.dsem = [self.nc.alloc_semaphore(name="d%d" % self.nd), 0, "d%d" % self.nd]
            self.dused.append(b.dsem)
        return b.dsem

    def dma(self, E, out, in_, sb, reads=(), writes=()):
        self._deps(E, reads, writes)
        ds = self._dsem(sb)
        ds[1] += 16
        E.eng.dma_start(out=out, in_=in_).then_inc(ds[0], 16)
        tok = (ds[0], ds[1], ds[2], "dma")
        self._mark(tok, reads, writes)
        return tok

    def barrier(self):
        toks = []
        for F in self.engs:
            if F.sem is not None and F.cnt > 0:
                toks.append((F.sem, F.cnt, F.name, F.name + "_bar"))
        for ds in self.dused:
            if ds[1] > 0:
                toks.append((ds[0], ds[1], ds[2], "dma"))
        for E in self.engs:
            for t in toks:
                if t[2] == E.name:
                    continue
                E.wait(t)
        for ds in self.dused:
            self.dpool.append(ds)
        self.dused = []
        for b in self.phase_bufs:
            b.w, b.r, b.dsem = {}, {}, None
        self.phase_bufs = []


def na_rs(r, rows):
    return min(max(r - 4, 0), rows - 8)


def build(T=8192, L=4, NS=2, only=None):
    nc = bass.Bass("TRN2", target_bir_lowering=False)
    k = K(nc)
    PE, ACT, DVE, POOL, SP = k.pe, k.act, k.dve, k.pool, k.sp
    ROWS = T // 64
    NT = T // NTOK
    NG = ROWS // 8
    N = NTOK

    def din(name, shape, dt=F32):
        return nc.dram_tensor(name, list(shape), dt, kind="ExternalInput").ap()

    def dscr(name, shape, dt):
        return nc.dram_tensor(name, list(shape), dt, kind="Internal").ap()

    x0 = din("x0", [NS, T, D])
    mem0 = din("mem0", [NS, NMEM, D])
    vecs = din("vecs", [L, 128, NVEC])
    w_up1 = din("w_ffn1_up", [L, D, 2 * FF]); w_dn1 = din("w_ffn1_down", [L, FF, D])
    w_up2 = din("w_ffn2_up", [L, D, 2 * FF]); w_dn2 = din("w_ffn2_down", [L, FF, D])
    w_in = din("w_in", [L, D, 3072]); w_gate = din("w_gate", [L, D, 3072])
    w_kv = din("w_mem_kv", [L, D, 1024])
    w_bna = din("w_branch_na", [L, 512, D]); w_blru = din("w_branch_lru", [L, 512, D])
    w_bca = din("w_branch_ca", [L, 512, D]); w_out = din("w_out", [L, D, D])
    lru_wa = din("lru_wa", [L, 2, 8, 64, 64]); lru_wi = din("lru_wi", [L, 2, 8, 64, 64])
    rpbg = din("rpbg", [L, H_NA, 128, 1024])
    cmask = din("cmask", [128, 1024])
    cident = din("cident", [128, 128])
    yout = nc.dram_tensor("yout", [NS, T, D], F32, kind="ExternalOutput").ap()

    xT = dscr("xT", [NS, D, T], F32)
    qT = dscr("qT", [NS, 512, T], BF16); kT = dscr("kT", [NS, 512, T], BF16)
    Vt = dscr("Vt", [NS, T, 512], BF16)
    xlT = dscr("xlT", [NS, 512, T], F32); glT = dscr("glT", [NS, 512, T], F32)
    gtT = dscr("gtT", [NS, 3072, T], BF16)
    ycaT = dscr("ycaT", [NS, 512, T], BF16); ylruT = dscr("ylruT", [NS, 512, T], BF16)
    ynaT = dscr("ynaT", [NS, 512, T], BF16)

    cast_rr = [0]
    uctr = [0]

    def un(n):
        uctr[0] += 1
        return "%s_%d" % (n, uctr[0])

    def cast(out, in_, reads, writes):
        i = cast_rr[0] % 3
        cast_rr[0] += 1
        if i == 0:
            return k.op(ACT, lambda e: e.activation(out=out, in_=in_, func=AF.Copy), reads, writes)
        if i == 1:
            return k.op(DVE, lambda e: e.tensor_copy(out=out, in_=in_), reads, writes)
        return k.op(POOL, lambda e: e.tensor_copy(out=out, in_=in_), reads, writes)

    with nc.psum_tensor("ps", [128, 8, 512], F32) as ps, \
            nc.sbuf_tensor("vec", [128, NVEC], F32) as vec, \
            nc.sbuf_tensor("identf", [128, 128], F32) as identf, \
            nc.sbuf_tensor("identb", [128, 128], BF16) as identb, \
            nc.sbuf_tensor("onesm", [128, 128], BF16) as onesm, \
            nc.sbuf_tensor("onesb", [128, 128], BF16) as onesb, \
            nc.sbuf_tensor("zerob", [128, 128], BF16) as zerob:

        cb = k.buf("const")
        k.dma(SP, identf[:], cident[:, :], cb, writes=[cb])
        k.op(DVE, lambda e: e.tensor_copy(out=identb[:], in_=identf[:]), reads=[cb], writes=[cb])
        k.op(DVE, lambda e: e.memset(onesm[:], 1.0 / 1024.0), writes=[cb])
        k.op(DVE, lambda e: e.memset(onesb[:], 1.0), writes=[cb])
        k.op(DVE, lambda e: e.memset(zerob[:], 0.0), writes=[cb])
        k.barrier()

        def load_vec(l):
            b = k.buf("vec")
            k.dma(SP, vec[:], vecs[l], b, writes=[b])
            return b

        def load_weight(dst3, src2, stage, stage_bufs, rows=128):
            nk, ncol = dst3.shape[1], dst3.shape[2]
            SW = stage.shape[2]
            idx = load_weight.idx
            for kk in range(nk):
                for c0 in range(0, ncol, SW):
                    cw = min(SW, ncol - c0)
                    s = idx % len(stage_bufs)
                    idx += 1
                    sb = stage_bufs[s]
                    k.dma(SP, stage[0:rows, s, 0:cw], src2[kk * rows:(kk + 1) * rows, c0:c0 + cw], sb, writes=[sb])
                    cast(dst3[:, kk, c0:c0 + cw], stage[0:rows, s, 0:cw], [sb], [])
            load_weight.idx = idx
        load_weight.idx = 0

        def rmsnorm(xv, xb, hs, hsb, gcol, psn, psnb, rs, rsb, n):
            k.op(POOL, lambda e: e.tensor_tensor(out=hs, in0=xv, in1=xv, op=ALU.mult), reads=[xb], writes=[hsb])
            for c in range(KC):
                k.op(PE, lambda e, c=c: e.matmul(psn, lhsT=onesm[:, :], rhs=hs[:, c, :], start=(c == 0), stop=(c == KC - 1)),
                     reads=[hsb], writes=[psnb])
            k.op(ACT, lambda e: e.activation(out=rs, in_=psn, func=AF.Sqrt, bias=EPS, scale=1.0), reads=[psnb], writes=[rsb])
            k.op(DVE, lambda e: e.reciprocal(out=rs, in_=rs), reads=[rsb], writes=[rsb])
            for c in range(KC):
                k.op(DVE, lambda e, c=c: e.scalar_tensor_tensor(out=hs[:, c, :], in0=xv[:, c, :], scalar=gcol[:, c:c + 1], in1=rs,
                                                               op0=ALU.mult, op1=ALU.mult), reads=[xb, rsb], writes=[hsb])

        def postnorm_residual(y, yb, xv, xb, hs, hsb, gcol, psn, psnb, rs, rsb, half):
            k.op(POOL, lambda e: e.tensor_tensor(out=hs, in0=y, in1=y, op=ALU.mult), reads=[yb], writes=[hsb])
            for c in range(KC):
                k.op(PE, lambda e, c=c: e.matmul(psn, lhsT=onesm[:, :], rhs=hs[:, c, :], start=(c == 0), stop=(c == KC - 1)),
                     reads=[hsb], writes=[psnb])
            k.op(ACT, lambda e: e.activation(out=rs, in_=psn, func=AF.Sqrt, bias=EPS, scale=1.0), reads=[psnb], writes=[rsb])
            k.op(DVE, lambda e: e.reciprocal(out=rs, in_=rs), reads=[rsb], writes=[rsb])
            for c in range(KC):
                k.op(DVE, lambda e, c=c: e.scalar_tensor_tensor(out=y[:, c, :], in0=y[:, c, :], scalar=gcol[:, c:c + 1], in1=rs,
                                                               op0=ALU.mult, op1=ALU.mult), reads=[yb, rsb], writes=[yb])
                if half:
                    k.op(DVE, lambda e, c=c: e.scalar_tensor_tensor(out=xv[:, c, :], in0=y[:, c, :], scalar=0.5, in1=xv[:, c, :],
                                                                   op0=ALU.mult, op1=ALU.add), reads=[yb, xb], writes=[xb])
                else:
                    k.op(DVE, lambda e, c=c: e.tensor_tensor(out=xv[:, c, :], in0=y[:, c, :], in1=xv[:, c, :], op=ALU.add),
                         reads=[yb, xb], writes=[xb])

        def xT_tile(s, t0, n):
            return xT[s].rearrange("(c p) t -> p c t", p=128)[:, :, t0:t0 + n]

        def phase_ffn(l, w_up, w_dn, gpre, gpost, first, last):
            with nc.sbuf_tensor(un("wup"), [128, KC, 2 * FF], BF16) as wup, \
                    nc.sbuf_tensor(un("wdn"), [128, FC, D], BF16) as wdn, \
                    nc.sbuf_tensor(un("xx"), [128, 2, KC, N], F32) as xx, \
                    nc.sbuf_tensor(un("hs"), [128, KC, N], BF16) as hs, \
                    nc.sbuf_tensor(un("uu"), [128, FC, N], BF16) as uu, \
                    nc.sbuf_tensor(un("yy"), [128, KC, N], F32) as yy, \
                    nc.sbuf_tensor(un("rs"), [128, 2, N], F32) as rs, \
                    nc.sbuf_tensor(un("st"), [128, 2, N], F32) as st, \
                    nc.sbuf_tensor(un("tm"), [128, 2, D], F32) as tm:
                vb = load_vec(l)
                stage = xx[:].rearrange("p a c n -> p (a c n)").rearrange("p (s w) -> p s w", s=4)
                sbs = k.bufs(4, "stg")
                load_weight(wup[:], w_up[l], stage, sbs)
                load_weight(wdn[:], w_dn[l], stage, sbs)
                k.barrier()
                xb = k.bufs(2, "x"); hsb = k.buf("hs"); ub = k.bufs(FC, "u"); yb = k.buf("y")
                rsb = k.bufs(2, "rs"); stb = k.bufs(2, "st"); tmb = k.buf("tm")
                pab = k.bufs(2, "pab"); pdb = k.bufs(2, "pd"); pnb = k.buf("pn"); ptb = k.bufs(2, "pt")
                it = 0
                for s in range(NS):
                    for ti in range(NT):
                        t0 = ti * N
                        sl = it % 2
                        it += 1
                        xv = xx[:, sl]
                        if first:
                            k.dma(SP, tm[:], x0[s, t0:t0 + N, :].rearrange("(a p) d -> p a d", p=128), tmb, writes=[tmb])
                            for bi in range(4):
                                pt = ps[:, 5 + bi % 2, :]
                                for cc in range(2):
                                    for a in range(2):
                                        c = 2 * bi + cc
                                        k.op(PE, lambda e, c=c, a=a, cc=cc: e.transpose(
                                            out=pt[:, cc * 256 + a * 128: cc * 256 + a * 128 + 128],
                                            in_=tm[:, a, c * 128:(c + 1) * 128], identity=identf[:]),
                                             reads=[tmb], writes=[ptb[bi % 2]])
                                dst = xv[:, 2 * bi:2 * bi + 2, :].rearrange("p c n -> p (c n)")
                                if bi % 2 == 0:
                                    k.op(ACT, lambda e, dst=dst, pt=pt: e.activation(out=dst, in_=pt, func=AF.Copy),
                                         reads=[ptb[bi % 2]], writes=[xb[sl]])
                                else:
                                    k.op(DVE, lambda e, dst=dst, pt=pt: e.tensor_copy(out=dst, in_=pt),
                                         reads=[ptb[bi % 2]], writes=[xb[sl]])
                        else:
                            k.dma(SP, xv, xT_tile(s, t0, N), xb[sl], writes=[xb[sl]])
                        rmsnorm(xv, xb[sl], hs[:], hsb, vec[:, gpre:gpre + 8], ps[:, 4, 0:N], pnb, rs[:, 0, :], rsb[0], N)
                        for j in range(FC):
                            pab_t = ps[:, j % 2, :]
                            for half in range(2):
                                for kk in range(KC):
                                    k.op(PE, lambda e, kk=kk, half=half, j=j: e.matmul(
                                        pab_t[:, half * N:(half + 1) * N],
                                        lhsT=wup[:, kk, half * FF + j * 128: half * FF + (j + 1) * 128],
                                        rhs=hs[:, kk, :], start=(kk == 0), stop=(kk == KC - 1)),
                                         reads=[hsb], writes=[pab[j % 2]])
                            k.op(ACT, lambda e, j=j: e.activation(out=st[:, j % 2, :], in_=pab_t[:, 0:N], func=AF.Silu),
                                 reads=[pab[j % 2]], writes=[stb[j % 2]])
                            k.op(DVE, lambda e, j=j: e.tensor_tensor(out=uu[:, j, :], in0=pab_t[:, N:2 * N], in1=st[:, j % 2, :], op=ALU.mult),
                                 reads=[pab[j % 2], stb[j % 2]], writes=[ub[j]])
                        for oc in range(KC):
                            pd = ps[:, 2 + oc % 2, 0:N]
                            for kk in range(FC):
                                k.op(PE, lambda e, kk=kk, oc=oc: e.matmul(pd, lhsT=wdn[:, kk, oc * 128:(oc + 1) * 128], rhs=uu[:, kk, :],
                                                                      start=(kk == 0), stop=(kk == FC - 1)),
                                     reads=[ub[kk]], writes=[pdb[oc % 2]])
                            if oc % 2 == 0:
                                k.op(ACT, lambda e, oc=oc: e.activation(out=yy[:, oc, :], in_=pd, func=AF.Copy), reads=[pdb[oc % 2]], writes=[yb])
                            else:
                                k.op(DVE, lambda e, oc=oc: e.tensor_copy(out=yy[:, oc, :], in_=pd), reads=[pdb[oc % 2]], writes=[yb])
                        postnorm_residual(yy[:], yb, xv, xb[sl], hs[:], hsb, vec[:, gpost:gpost + 8], ps[:, 4, 0:N], pnb,
                                          rs[:, 1, :], rsb[1], True)
                        if last:
                            for a in range(2):
                                for bi in range(2):
                                    pt = ps[:, 5 + bi % 2, :]
                                    for cc in range(4):
                                        c = 4 * bi + cc
                                        k.op(PE, lambda e, c=c, a=a, cc=cc: e.transpose(
                                            out=pt[:, cc * 128:(cc + 1) * 128], in_=xv[:, c, a * 128:(a + 1) * 128], identity=identf[:]),
                                             reads=[xb[sl]], writes=[ptb[bi % 2]])
                                    dst = tm[:, a, bi * 512:(bi + 1) * 512]
                                    if bi % 2 == 0:
                                        k.op(ACT, lambda e, dst=dst, pt=pt: e.activation(out=dst, in_=pt, func=AF.Copy),
                                             reads=[ptb[bi % 2]], writes=[tmb])
                                    else:
                                        k.op(DVE, lambda e, dst=dst, pt=pt: e.tensor_copy(out=dst, in_=pt),
                                             reads=[ptb[bi % 2]], writes=[tmb])
                            k.dma(POOL, yout[s, t0:t0 + N, :].rearrange("(a p) d -> p a d", p=128), tm[:], tmb, reads=[tmb])
                        else:
                            k.dma(POOL, xT_tile(s, t0, N), xv, xb[sl], reads=[xb[sl]])
                k.barrier()

        def phase_mixin(l):
            with nc.sbuf_tensor(un("win"), [128, KC, 3072], BF16) as win, \
                    nc.sbuf_tensor(un("wg"), [128, KC, 3072], BF16) as wg, \
                    nc.sbuf_tensor(un("wkv"), [128, KC, 1024], BF16) as wkv, \
                    nc.sbuf_tensor(un("kca"), [128, NS, 4, NMEM], BF16) as kca, \
                    nc.sbuf_tensor(un("vca"), [128, NS, 2, 512], BF16) as vca, \
                    nc.sbuf_tensor(un("xx"), [128, 2, KC, N], F32) as xx, \
                    nc.sbuf_tensor(un("hs"), [128, KC, N], BF16) as hs, \
                    nc.sbuf_tensor(un("rs"), [128, 2, N], F32) as rs, \
                    nc.sbuf_tensor(un("qk"), [128, 8, N], BF16) as qk, \
                    nc.sbuf_tensor(un("vtok"), [128, 2, 512], BF16) as vtok, \
                    nc.sbuf_tensor(un("xl"), [128, 4, N], F32) as xl, \
                    nc.sbuf_tensor(un("gl"), [128, 4, N], F32) as gl, \
                    nc.sbuf_tensor(un("qca"), [128, 4, N], BF16) as qca, \
                    nc.sbuf_tensor(un("gt"), [128, 24, N], BF16) as gt, \
                    nc.sbuf_tensor(un("pT"), [128, 2, 2, N], BF16) as pT, \
                    nc.sbuf_tensor(un("yca"), [128, 4, N], BF16) as yca, \
                    nc.sbuf_tensor(un("tm"), [128, 2, D], F32) as tm:
                vb = load_vec(l)
                stage = xx[:].rearrange("p a c n -> p (a c n)").rearrange("p (s w) -> p s w", s=4)
                sbs = k.bufs(4, "stg")
                load_weight(win[:], w_in[l], stage, sbs)
                load_weight(wg[:], w_gate[l], stage, sbs)
                load_weight(wkv[:], w_kv[l], stage, sbs)
                k.barrier()
                xb = k.bufs(2, "x"); hsb = k.buf("hs"); rsb = k.bufs(2, "rs"); tmb = k.buf("tm")
                pb = k.bufs(8, "psb")
                for s in range(NS):
                    k.dma(SP, tm[:], mem0[s].rearrange("(a p) d -> p a d", p=128), tmb, writes=[tmb])
                    xv = xx[:, 0]
                    for bi in range(4):
                        pt = ps[:, 5 + bi % 2, :]
                        for cc in range(2):
                            for a in range(2):
                                c = 2 * bi + cc
                                k.op(PE, lambda e, c=c, a=a, cc=cc: e.transpose(
                                    out=pt[:, cc * 256 + a * 128: cc * 256 + a * 128 + 128],
                                    in_=tm[:, a, c * 128:(c + 1) * 128], identity=identf[:]),
                                     reads=[tmb], writes=[pb[5 + bi % 2]])
                        dst = xv[:, 2 * bi:2 * bi + 2, :].rearrange("p c n -> p (c n)")
                        k.op(DVE, lambda e, dst=dst, pt=pt: e.tensor_copy(out=dst, in_=pt), reads=[pb[5 + bi % 2]], writes=[xb[0]])
                    rmsnorm(xv, xb[0], hs[:], hsb, vec[:, V_GMEM:V_GMEM + 8], ps[:, 4, 0:N], pb[4], rs[:, 0, :], rsb[0], N)
                    for h in range(4):
                        pp = ps[:, h % 2, 0:NMEM]
                        for c in range(KC):
                            k.op(PE, lambda e, c=c, h=h: e.matmul(pp, lhsT=wkv[:, c, h * 128:(h + 1) * 128], rhs=hs[:, c, :],
                                                              start=(c == 0), stop=(c == KC - 1)), reads=[hsb], writes=[pb[h % 2]])
                        k.op(DVE, lambda e, h=h: e.tensor_copy(out=kca[:, s, h, :], in_=pp), reads=[pb[h % 2]], writes=[tmb])
                    for mc in range(2):
                        pp = ps[:, 2 + mc, :]
                        for c in range(KC):
                            k.op(PE, lambda e, c=c, mc=mc: e.matmul(pp, lhsT=hs[:, c, mc * 128:(mc + 1) * 128], rhs=wkv[:, c, 512:1024],
                                                                start=(c == 0), stop=(c == KC - 1)), reads=[hsb], writes=[pb[2 + mc]])
                        k.op(ACT, lambda e, mc=mc: e.activation(out=vca[:, s, mc, :], in_=pp, func=AF.Copy), reads=[pb[2 + mc]], writes=[tmb])
                k.barrier()
                xb = k.bufs(2, "x"); hsb = k.buf("hs"); rsb = k.bufs(2, "rs")
                pb = k.bufs(8, "psb")
                qkb = k.buf("qk"); vtb = k.buf("vt"); xlb = k.buf("xl"); glb = k.buf("gl"); qcb = k.bufs(4, "qca")
                gtb = k.bufs(4, "gt"); pTb = k.bufs(2, "pT"); ycb = k.buf("yca"); recb = k.buf("rec")
                it = 0
                bank = [0]

                def nb():
                    b = bank[0] % 4
                    bank[0] += 1
                    return b
                for s in range(NS):
                    for ti in range(NT):
                        t0 = ti * N
                        sl = it % 2
                        it += 1
                        xv = xx[:, sl]
                        k.dma(SP, xv, xT_tile(s, t0, N), xb[sl], writes=[xb[sl]])
                        rmsnorm(xv, xb[sl], hs[:], hsb, vec[:, V_GMPRE:V_GMPRE + 8], ps[:, 4, 0:N], pb[4], rs[:, 0, :], rsb[0], N)
                        for oc in list(range(0, 8)) + list(range(12, 24)):
                            b = nb()
                            pp = ps[:, b, 0:N]
                            for c in range(KC):
                                k.op(PE, lambda e, c=c, oc=oc: e.matmul(pp, lhsT=win[:, c, oc * 128:(oc + 1) * 128], rhs=hs[:, c, :],
                                                                    start=(c == 0), stop=(c == KC - 1)), reads=[hsb], writes=[pb[b]])
                            if oc < 4:
                                k.op(ACT, lambda e, oc=oc: e.activation(out=qk[:, oc, :], in_=pp, func=AF.Copy, scale=0.125),
                                     reads=[pb[b]], writes=[qkb])
                            elif oc < 8:
                                k.op(DVE, lambda e, oc=oc: e.tensor_copy(out=qk[:, oc, :], in_=pp), reads=[pb[b]], writes=[qkb])
                            elif oc < 16:
                                k.op(ACT, lambda e, oc=oc: e.activation(out=xl[:, oc - 12, :], in_=pp, func=AF.Copy), reads=[pb[b]], writes=[xlb])
                            elif oc < 20:
                                k.op(ACT, lambda e, oc=oc: e.activation(out=gl[:, oc - 16, :], in_=pp, func=AF.Gelu), reads=[pb[b]], writes=[glb])
                            else:
                                k.op(DVE, lambda e, oc=oc: e.tensor_scalar(out=qca[:, oc - 20, :], in0=pp, scalar1=float(128 ** -0.5), scalar2=None,
                                                                       op0=ALU.mult), reads=[pb[b]], writes=[qcb[oc - 20]])
                        k.dma(POOL, qT[s].rearrange("(c p) t -> p c t", p=128)[:, :, t0:t0 + N], qk[:, 0:4, :], qkb, reads=[qkb])
                        k.dma(POOL, kT[s].rearrange("(c p) t -> p c t", p=128)[:, :, t0:t0 + N], qk[:, 4:8, :], qkb, reads=[qkb])
                        k.dma(POOL, xlT[s].rearrange("(c p) t -> p c t", p=128)[:, :, t0:t0 + N], xl[:], xlb, reads=[xlb])
                        k.dma(POOL, glT[s].rearrange("(c p) t -> p c t", p=128)[:, :, t0:t0 + N], gl[:], glb, reads=[glb])
                        for a in range(2):
                            b = nb()
                            pp = ps[:, b, :]
                            for c in range(KC):
                                k.op(PE, lambda e, c=c, a=a: e.matmul(pp, lhsT=hs[:, c, a * 128:(a + 1) * 128], rhs=win[:, c, 1024:1536],
                                                                  start=(c == 0), stop=(c == KC - 1)), reads=[hsb], writes=[pb[b]])
                            k.op(DVE, lambda e, a=a: e.tensor_copy(out=vtok[:, a, :], in_=pp), reads=[pb[b]], writes=[vtb])
                        k.dma(POOL, Vt[s, t0:t0 + N, :].rearrange("(a p) f -> p a f", p=128), vtok[:], vtb, reads=[vtb])
                        for h in range(4):
                            for mc in range(2):
                                b = nb()
                                pp = ps[:, b, 0:N]
                                k.op(PE, lambda e, h=h, mc=mc: e.matmul(pp, lhsT=kca[:, s, h, mc * 128:(mc + 1) * 128], rhs=qca[:, h, :],
                                                                    start=True, stop=True), reads=[qcb[h]], writes=[pb[b]])
                                k.op(ACT, lambda e, h=h, mc=mc: e.activation(out=pT[:, h % 2, mc, :], in_=pp, func=AF.Exp),
                                     reads=[pb[b]], writes=[pTb[h % 2]])
                            b = nb()
                            po = ps[:, b, :]
                            for mc in range(2):
                                k.op(PE, lambda e, h=h, mc=mc: e.matmul(po[:, 0:N], lhsT=vca[:, s, mc, h * 128:(h + 1) * 128], rhs=pT[:, h % 2, mc, :],
                                                                    start=(mc == 0), stop=(mc == 1)), reads=[pTb[h % 2]], writes=[pb[b]])
                            for mc in range(2):
                                k.op(PE, lambda e, h=h, mc=mc: e.matmul(po[:, N:2 * N], lhsT=onesb[:, :], rhs=pT[:, h % 2, mc, :],
                                                                    start=(mc == 0), stop=(mc == 1)), reads=[pTb[h % 2]], writes=[pb[b]])
                            k.op(DVE, lambda e: e.reciprocal(out=rs[:, 1, :], in_=po[:, N:2 * N]), reads=[pb[b]], writes=[recb])
                            k.op(DVE, lambda e, h=h: e.tensor_tensor(out=yca[:, h, :], in0=po[:, 0:N], in1=rs[:, 1, :], op=ALU.mult),
                                 reads=[pb[b], recb], writes=[ycb])
                        k.dma(POOL, ycaT[s].rearrange("(c p) t -> p c t", p=128)[:, :, t0:t0 + N], yca[:], ycb, reads=[ycb])
                        for oc in range(24):
                            b = nb()
                            pp = ps[:, b, 0:N]
                            for c in range(KC):
                                k.op(PE, lambda e, c=c, oc=oc: e.matmul(pp, lhsT=wg[:, c, oc * 128:(oc + 1) * 128], rhs=hs[:, c, :],
                                                                    start=(c == 0), stop=(c == KC - 1)), reads=[hsb], writes=[pb[b]])
                            k.op(ACT, lambda e, oc=oc: e.activation(out=gt[:, oc, :], in_=pp, func=AF.Sigmoid, bias=vec[:, V_BG + oc:V_BG + oc + 1],
                                                                 scale=1.0), reads=[pb[b]], writes=[gtb[oc // 6]])
                            if oc % 6 == 5:
                                q = oc // 6
                                k.dma(POOL, gtT[s].rearrange("(c p) t -> p c t", p=128)[:, 6 * q:6 * q + 6, t0:t0 + N], gt[:, 6 * q:6 * q + 6, :],
                                      gtb[q], reads=[gtb[q]])
                k.barrier()

        def phase_na(l):
            with nc.sbuf_tensor(un("qe"), [128, 2, 4, 512], BF16) as qe, \
                    nc.sbuf_tensor(un("qo"), [128, 2, 4, 512], BF16) as qo, \
                    nc.sbuf_tensor(un("kg"), [128, 2, 4, 1024], BF16) as kg, \
                    nc.sbuf_tensor(un("ve"), [128, 2, 8, 512], BF16) as ve, \
                    nc.sbuf_tensor(un("vo"), [128, 2, 8, 512], BF16) as vo, \
                    nc.sbuf_tensor(un("t2"), [128, H_NA, 1024], BF16) as t2, \
                    nc.sbuf_tensor(un("msk"), [128, 1024], F32) as msk, \
                    nc.sbuf_tensor(un("stg"), [128, 2, 1024], F32) as stg, \
                    nc.sbuf_tensor(un("pT"), [128, 3, 512], BF16) as pT, \
                    nc.sbuf_tensor(un("rec"), [128, 2, 512], F32) as rec, \
                    nc.sbuf_tensor(un("onesE"), [128, 128], BF16) as onesE, \
                    nc.sbuf_tensor(un("onesO"), [128, 128], BF16) as onesO, \
                    nc.sbuf_tensor(un("yna"), [128, 2, 4, 512], BF16) as yna:
                mb = k.buf("msk"); sbs = k.bufs(2, "stg"); t2b = k.buf("t2")
                k.dma(SP, msk[:], cmask[:, :], mb, writes=[mb])
                for h in range(H_NA):
                    sb = sbs[h % 2]
                    k.dma(SP, stg[:, h % 2, :], rpbg[l, h], sb, writes=[sb])
                    k.op(DVE, lambda e, h=h: e.tensor_tensor(out=t2[:, h, :], in0=stg[:, h % 2, :], in1=msk[:], op=ALU.add),
                         reads=[sb, mb], writes=[t2b])
                zb = k.buf("z")
                k.op(POOL, lambda e: e.memset(qe[:], 0.0), writes=[zb])
                k.op(POOL, lambda e: e.memset(qo[:], 0.0), writes=[zb])
                k.op(POOL, lambda e: e.memset(ve[:], 0.0), writes=[zb])
                k.op(POOL, lambda e: e.memset(vo[:], 0.0), writes=[zb])
                k.op(DVE, lambda e: e.memset(onesE[:], 0.0), writes=[zb])
                k.op(DVE, lambda e: e.memset(onesO[:], 0.0), writes=[zb])
                k.op(DVE, lambda e: e.memset(onesE[0:64, :], 1.0), writes=[zb])
                k.op(DVE, lambda e: e.memset(onesO[64:128, :], 1.0), writes=[zb])
                k.barrier()
                qb = k.bufs(2, "qg"); kb = k.bufs(2, "kg"); vb_ = k.bufs(2, "vg")
                pTb = k.bufs(3, "pT"); recb = k.bufs(2, "rec"); ynb = k.bufs(2, "yna")
                psb = k.bufs(2, "pst"); pob = k.bufs(2, "po"); pqb = k.bufs(2, "pq")
                it = 0
                ipair = 0
                ih = 0
                qv = lambda s: qT[s].rearrange("(c p) t -> p c t", p=128)
                for s in range(NS):
                    for g in range(NG):
                        sl = it % 2
                        it += 1
                        r0 = 8 * g
                        kr_lo = na_rs(r0, ROWS)
                        kr_hi = na_rs(r0 + 7, ROWS) + 7
                        p_lo, p_hi = kr_lo // 2, kr_hi // 2
                        npair = p_hi - p_lo + 1
                        k.dma(SP, qe[0:64, sl], qv(s)[0:64, :, r0 * 64:r0 * 64 + 512], qb[sl], writes=[qb[sl]])
                        k.dma(SP, qo[64:128, sl], qv(s)[64:128, :, r0 * 64:r0 * 64 + 512], qb[sl], writes=[qb[sl]])
                        k.dma(SP, kg[:, sl, :, 0:npair * 128], kT[s].rearrange("(c p) t -> p c t", p=128)[:, :, p_lo * 128:(p_hi + 1) * 128],
                              kb[sl], writes=[kb[sl]])
                        vv = Vt[s, p_lo * 128:(p_hi + 1) * 128, :].rearrange("(m p) f -> p m f", p=128)
                        k.dma(SP, ve[0:64, sl, 0:npair, :], vv[0:64], vb_[sl], writes=[vb_[sl]])
                        k.dma(SP, vo[64:128, sl, 0:npair, :], vv[64:128], vb_[sl], writes=[vb_[sl]])
                        for h in range(H_NA):
                            c = h // 2
                            pbase = (h % 2) * 64
                            qz = qe if h % 2 == 0 else qo
                            hs_ = ih % 2
                            ih += 1
                            po = ps[:, 2 + hs_, :]
                            pq = ps[:, 4 + hs_, :]
                            k.op(PE, lambda e, po=po: e.matmul(po, lhsT=zerob[:, :], rhs=kg[:, sl, 0, 0:512], start=True, stop=False),
                                 reads=[kb[sl]], writes=[pob[hs_]])
                            k.op(PE, lambda e, pq=pq: e.matmul(pq, lhsT=zerob[:, :], rhs=kg[:, sl, 0, 0:512], start=True, stop=False),
                                 reads=[kb[sl]], writes=[pqb[hs_]])
                            items = []
                            for j in range(npair):
                                mp = p_lo + j
                                vr = []
                                for half in range(2):
                                    m = 2 * mp + half
                                    rr_ = [r for r in range(r0, r0 + 8) if na_rs(r, ROWS) <= m <= na_rs(r, ROWS) + 7]
                                    vr.append((rr_[0], rr_[-1]) if rr_ else None)
                                los = [v[0] for v in vr if v]
                                his = [v[1] for v in vr if v]
                                if not los:
                                    continue
                                items.append((j, mp, min(los), max(his), vr))
                            for ii, (j, mp, ra, rb, vr) in enumerate(items):
                                lastitem = ii == len(items) - 1
                                ncol = (rb - ra + 1) * 64
                                c0 = (ra - r0) * 64
                                s0 = ra - 2 * mp + 7
                                assert 0 <= s0 and s0 + (rb - ra) < 16, (s0, ra, rb, mp)
                                sb_ = ipair % 2
                                pi = ipair % 3
                                ipair += 1
                                pst = ps[:, sb_, 0:ncol]
                                k.op(PE, lambda e, pst=pst, j=j, c0=c0, ncol=ncol: e.matmul(
                                    pst, lhsT=kg[:, sl, c, j * 128:(j + 1) * 128],
                                    rhs=qz[:, sl, c, c0:c0 + ncol], start=True, stop=False),
                                     reads=[kb[sl], qb[sl]], writes=[psb[sb_]])
                                k.op(PE, lambda e, pst=pst, s0=s0, ncol=ncol: e.matmul(
                                    pst, lhsT=identb[:, :], rhs=t2[:, h, s0 * 64:s0 * 64 + ncol], start=False, stop=True),
                                     reads=[t2b], writes=[psb[sb_]])
                                k.op(ACT, lambda e, pst=pst, pi=pi, ncol=ncol: e.activation(out=pT[:, pi, 0:ncol], in_=pst, func=AF.Exp),
                                     reads=[psb[sb_]], writes=[pTb[pi]])
                                nhalf = [hf for hf in range(2) if vr[hf]]
                                for hi_, half in enumerate(nhalf):
                                    va, vb2 = vr[half]
                                    cl0 = (va - r0) * 64
                                    pc0 = (va - ra) * 64
                                    nv = (vb2 - va + 1) * 64
                                    fin = lastitem and hi_ == len(nhalf) - 1
                                    vz = ve if half == 0 else vo
                                    oz = onesE if half == 0 else onesO
                                    k.op(PE, lambda e, vz=vz, j=j, cl0=cl0, pc0=pc0, nv=nv, pi=pi, fin=fin: e.matmul(
                                        po[:, cl0:cl0 + nv], lhsT=vz[:, sl, j, c * 128:(c + 1) * 128],
                                        rhs=pT[:, pi, pc0:pc0 + nv], start=False, stop=fin),
                                         reads=[pTb[pi], vb_[sl]], writes=[pob[hs_]])
                                    k.op(PE, lambda e, oz=oz, cl0=cl0, pc0=pc0, nv=nv, pi=pi, fin=fin: e.matmul(
                                        pq[:, cl0:cl0 + nv], lhsT=oz[:, :],
                                        rhs=pT[:, pi, pc0:pc0 + nv], start=False, stop=fin),
                                         reads=[pTb[pi]], writes=[pqb[hs_]])
                            k.op(DVE, lambda e, pq=pq, hs_=hs_: e.reciprocal(out=rec[pbase:pbase + 64, hs_, :], in_=pq[pbase:pbase + 64, :]),
                                 reads=[pqb[hs_]], writes=[recb[hs_]])
                            k.op(DVE, lambda e, po=po, hs_=hs_, h=h: e.tensor_tensor(out=yna[pbase:pbase + 64, sl, c, :], in0=po[pbase:pbase + 64, :],
                                                                                  in1=rec[pbase:pbase + 64, hs_, :], op=ALU.mult),
                                 reads=[pob[hs_], recb[hs_]], writes=[ynb[sl]])
                        k.dma(POOL, ynaT[s].rearrange("(c p) t -> p c t", p=128)[:, :, r0 * 64:r0 * 64 + 512], yna[:, sl], ynb[sl], reads=[ynb[sl]])
                k.barrier()

        def phase_lru(l):
            PC = min(2048, T)
            NP = T // PC
            with nc.sbuf_tensor(un("xl"), [128, T], F32) as xl, \
                    nc.sbuf_tensor(un("xc"), [128, T], F32) as xc, \
                    nc.sbuf_tensor(un("xcb"), [128, T], BF16) as xcb, \
                    nc.sbuf_tensor(un("wst"), [128, 16, 128], F32) as wst, \
                    nc.sbuf_tensor(un("wbd"), [128, 16, 128], BF16) as wbd, \
                    nc.sbuf_tensor(un("sm"), [128, 64], F32) as sm, \
                    nc.sbuf_tensor(un("rr"), [128, PC], F32) as rr, \
                    nc.sbuf_tensor(un("ii"), [128, PC], F32) as ii_, \
                    nc.sbuf_tensor(un("aa"), [128, PC], F32) as aa, \
                    nc.sbuf_tensor(un("hb"), [128, 2, PC], F32) as hb, \
                    nc.sbuf_tensor(un("glp"), [128, 2, PC], F32) as glp, \
                    nc.sbuf_tensor(un("hsum"), [128, PC], F32) as hsum, \
                    nc.sbuf_tensor(un("yl"), [128, 2, PC], BF16) as yl:
                vb = load_vec(l)
                wb_ = k.buf("wbd")
                k.op(DVE, lambda e: e.memset(wst[:], 0.0), writes=[wb_])
                for d in range(2):
                    for gi, wsrc in enumerate((lru_wa, lru_wi)):
                        for jb in range(2):
                            dst = wst[jb * 64:(jb + 1) * 64, :, jb * 64:(jb + 1) * 64].rearrange("p (c x) m -> p c x m", x=4)[:, :, d * 2 + gi, :]
                            src = wsrc[l, d].rearrange("(c j) k m -> j k c m", j=2)[jb]
                            k.dma(SP, dst, src, wb_, writes=[wb_])
                k.op(DVE, lambda e: e.tensor_copy(out=wbd[:], in_=wst[:]), reads=[wb_], writes=[wb_])
                smb = k.buf("sm")
                lam = vec[:, V_LAM:V_LAM + 8]
                e_ = sm[:, 16:24]; es = sm[:, 24:32]; pp_ = sm[:, 32:40]; lnp = sm[:, 40:48]; mm_ = sm[:, 48:56]
                k.op(ACT, lambda e: e.activation(out=e_, in_=lam, func=AF.Exp, scale=-1.0), reads=[vb], writes=[smb])
                k.op(ACT, lambda e: e.activation(out=lnp, in_=e_, func=AF.Ln, bias=1.0, scale=1.0), reads=[smb], writes=[smb])
                k.op(DVE, lambda e: e.tensor_scalar(out=es, in0=e_, scalar1=0.1, scalar2=None, op0=ALU.min), reads=[smb], writes=[smb])
                k.op(DVE, lambda e: e.tensor_scalar(out=pp_, in0=es, scalar1=-1.0 / 6.0, scalar2=0.2, op0=ALU.mult, op1=ALU.add), reads=[smb], writes=[smb])
                for cf in (-0.25, 1.0 / 3.0, -0.5, 1.0):
                    k.op(DVE, lambda e: e.tensor_tensor(out=pp_, in0=pp_, in1=es, op=ALU.mult), reads=[smb], writes=[smb])
                    k.op(DVE, lambda e, cf=cf: e.tensor_scalar(out=pp_, in0=pp_, scalar1=float(cf), scalar2=None, op0=ALU.add), reads=[smb], writes=[smb])
                k.op(DVE, lambda e: e.tensor_tensor(out=pp_, in0=pp_, in1=es, op=ALU.mult), reads=[smb], writes=[smb])
                k.op(DVE, lambda e: e.tensor_scalar(out=mm_, in0=e_, scalar1=0.1, scalar2=None, op0=ALU.is_lt), reads=[smb], writes=[smb])
                k.op(DVE, lambda e: e.tensor_tensor(out=pp_, in0=pp_, in1=lnp, op=ALU.subtract), reads=[smb], writes=[smb])
                k.op(DVE, lambda e: e.tensor_tensor(out=pp_, in0=pp_, in1=mm_, op=ALU.mult), reads=[smb], writes=[smb])
                k.op(DVE, lambda e: e.tensor_tensor(out=pp_, in0=pp_, in1=lnp, op=ALU.add), reads=[smb], writes=[smb])
                k.op(DVE, lambda e: e.tensor_scalar(out=sm[:, 0:8], in0=pp_, scalar1=-8.0, scalar2=None, op0=ALU.mult), reads=[smb], writes=[smb])
                k.op(DVE, lambda e: e.tensor_scalar(out=sm[:, 8:16], in0=pp_, scalar1=-16.0, scalar2=None, op0=ALU.mult), reads=[smb], writes=[smb])
                k.barrier()
                xlb = k.buf("xl"); xcbuf = k.buf("xc"); xcbb = k.buf("xcb")
                rrb = k.buf("rr"); iib = k.buf("ii"); aab = k.buf("aa"); hbb = k.bufs(2, "hb"); glb = k.bufs(2, "gl"); ylb = k.bufs(2, "yl")
                prb = k.bufs(2, "pr"); pib = k.bufs(2, "pi"); hsb_ = k.buf("hsum")
                ip = 0
                for s in range(NS):
                    for c in range(4):
                        for q in range(NP):
                            k.dma(SP, xl[:, q * PC:(q + 1) * PC], xlT[s, c * 128:(c + 1) * 128, q * PC:(q + 1) * PC], xlb, writes=[xlb])
                        cw = lambda i: vec[:, V_CW + i * 4 + c: V_CW + i * 4 + c + 1]
                        k.op(DVE, lambda e: e.tensor_scalar(out=xc[:], in0=xl[:], scalar1=cw(2), scalar2=vec[:, V_CB + c:V_CB + c + 1],
                                                            op0=ALU.mult, op1=ALU.add), reads=[xlb], writes=[xcbuf])
                        for i in (0, 1, 3):
                            o = i - 2
                            d0, d1 = max(0, -o), T - max(0, o)
                            k.op(DVE, lambda e, i=i, o=o, d0=d0, d1=d1: e.scalar_tensor_tensor(
                                out=xc[:, d0:d1], in0=xl[:, d0 + o:d1 + o], scalar=cw(i), in1=xc[:, d0:d1], op0=ALU.mult, op1=ALU.add),
                                 reads=[xlb, xcbuf], writes=[xcbuf])
                        k.op(POOL, lambda e: e.tensor_copy(out=xcb[:], in_=xc[:]), reads=[xcbuf], writes=[xcbb])
                        hf = xl
                        for d in range(2):
                            order = range(NP) if d == 0 else range(NP - 1, -1, -1)
                            prev = None
                            for q in order:
                                t0 = q * PC
                                for sub in range(PC // 512):
                                    pr = ps[:, 0 + sub % 2, :]
                                    pi_ = ps[:, 2 + sub % 2, :]
                                    tt = t0 + sub * 512
                                    k.op(PE, lambda e, pr=pr, tt=tt: e.matmul(pr, lhsT=wbd[:, c * 4 + d * 2 + 0, :], rhs=xcb[:, tt:tt + 512], start=True, stop=True),
                                         reads=[xcbb], writes=[prb[sub % 2]])
                                    k.op(PE, lambda e, pi_=pi_, tt=tt: e.matmul(pi_, lhsT=wbd[:, c * 4 + d * 2 + 1, :], rhs=xcb[:, tt:tt + 512], start=True, stop=True),
                                         reads=[xcbb], writes=[pib[sub % 2]])
                                    k.op(ACT, lambda e, pr=pr, sub=sub: e.activation(out=rr[:, sub * 512:(sub + 1) * 512], in_=pr, func=AF.Sigmoid,
                                                                                 bias=vec[:, V_BA + d * 4 + c:V_BA + d * 4 + c + 1], scale=1.0),
                                         reads=[prb[sub % 2]], writes=[rrb])
                                    k.op(ACT, lambda e, pi_=pi_, sub=sub: e.activation(out=ii_[:, sub * 512:(sub + 1) * 512], in_=pi_, func=AF.Sigmoid,
                                                                                   bias=vec[:, V_BI + d * 4 + c:V_BI + d * 4 + c + 1], scale=1.0),
                                         reads=[pib[sub % 2]], writes=[iib])
                                k.op(ACT, lambda e: e.activation(out=aa[:], in_=rr[:], func=AF.Exp, scale=sm[:, d * 4 + c:d * 4 + c + 1]),
                                     reads=[rrb], writes=[aab])
                                k.op(ACT, lambda e: e.activation(out=rr[:], in_=rr[:], func=AF.Exp, scale=sm[:, 8 + d * 4 + c:8 + d * 4 + c + 1]),
                                     reads=[rrb], writes=[rrb])
                                k.op(ACT, lambda e: e.activation(out=rr[:], in_=rr[:], func=AF.Sqrt, bias=1.0, scale=-1.0), reads=[rrb], writes=[rrb])
                                k.op(DVE, lambda e, t0=t0: e.tensor_tensor(out=ii_[:], in0=ii_[:], in1=xc[:, t0:t0 + PC], op=ALU.mult),
                                     reads=[iib, xcbuf], writes=[iib])
                                k.op(DVE, lambda e: e.tensor_tensor(out=ii_[:], in0=ii_[:], in1=rr[:], op=ALU.mult), reads=[iib, rrb], writes=[iib])
                                if d == 0:
                                    init = 0.0 if prev is None else hf[:, t0 - 1:t0]
                                    k.op(DVE, lambda e, t0=t0, init=init: e.tensor_tensor_scan(out=hf[:, t0:t0 + PC], data0=aa[:], data1=ii_[:], initial=init,
                                                                                             op0=ALU.mult, op1=ALU.add), reads=[aab, iib, xlb], writes=[xlb])
                                else:
                                    sl = ip % 2
                                    ip += 1
                                    k.dma(SP, glp[:, sl, :], glT[s, c * 128:(c + 1) * 128, t0:t0 + PC], glb[sl], writes=[glb[sl]])
                                    init = 0.0 if prev is None else hb[:, prev, 0:1]
                                    rdeps = [aab, iib] + ([hbb[prev]] if prev is not None else [])
                                    k.op(DVE, lambda e, sl=sl, init=init: e.tensor_tensor_scan(out=hb[:, sl, ::-1], data0=aa[:, ::-1], data1=ii_[:, ::-1], initial=init,
                                                                                             op0=ALU.mult, op1=ALU.add), reads=rdeps, writes=[hbb[sl]])
                                    k.op(POOL, lambda e, sl=sl, t0=t0: e.tensor_tensor(out=hsum[:], in0=hb[:, sl, :], in1=hf[:, t0:t0 + PC], op=ALU.add),
                                         reads=[hbb[sl], xlb], writes=[hsb_])
                                    k.op(POOL, lambda e, sl=sl: e.tensor_tensor(out=yl[:, sl, :], in0=hsum[:], in1=glp[:, sl, :], op=ALU.mult),
                                         reads=[hsb_, glb[sl]], writes=[ylb[sl]])
                                    k.dma(POOL, ylruT[s, c * 128:(c + 1) * 128, t0:t0 + PC], yl[:, sl, :], ylb[sl], reads=[ylb[sl]])
                                    prev = sl
                                if d == 0:
                                    prev = 0
                k.barrier()

        def phase_merge(l):
            with nc.sbuf_tensor(un("wna"), [128, 4, D], BF16) as wna, \
                    nc.sbuf_tensor(un("wlr"), [128, 4, D], BF16) as wlr, \
                    nc.sbuf_tensor(un("wca"), [128, 4, D], BF16) as wca, \
                    nc.sbuf_tensor(un("wo"), [128, KC, D], BF16) as wo, \
                    nc.sbuf_tensor(un("xx"), [128, 2, KC, N], F32) as xx, \
                    nc.sbuf_tensor(un("yna"), [128, 2, 4, N], BF16) as yna, \
                    nc.sbuf_tensor(un("ylr"), [128, 2, 4, N], BF16) as ylr, \
                    nc.sbuf_tensor(un("yca"), [128, 2, 4, N], BF16) as yca, \
                    nc.sbuf_tensor(un("gt"), [128, 2, 24, N], BF16) as gt, \
                    nc.sbuf_tensor(un("tmp"), [128, 3, N], F32) as tmp, \
                    nc.sbuf_tensor(un("mgb"), [128, KC, N], BF16) as mgb, \
                    nc.sbuf_tensor(un("yy"), [128, KC, N], F32) as yy, \
                    nc.sbuf_tensor(un("hs"), [128, KC, N], BF16) as hs, \
                    nc.sbuf_tensor(un("rs"), [128, 2, N], F32) as rs:
                vb = load_vec(l)
                stage = xx[:].rearrange("p a c n -> p (a c n)").rearrange("p (s w) -> p s w", s=4)
                sbs = k.bufs(4, "stg")
                load_weight(wna[:], w_bna[l], stage, sbs)
                load_weight(wlr[:], w_blru[l], stage, sbs)
                load_weight(wca[:], w_bca[l], stage, sbs)
                load_weight(wo[:], w_out[l], stage, sbs)
                k.barrier()
                xb = k.bufs(2, "x"); ynb = k.bufs(2, "yna"); ylb = k.bufs(2, "ylr"); ycb = k.bufs(2, "yca"); gtb = k.bufs(2, "gt")
                tb = k.bufs(3, "tmp"); mgbb = k.bufs(KC, "mg"); yb = k.buf("y"); hsb = k.buf("hs"); rsb = k.buf("rs")
                pb = k.bufs(8, "psb")
                it = 0
                for s in range(NS):
                    for ti in range(NT):
                        t0 = ti * N
                        sl = it % 2
                        it += 1
                        xv = xx[:, sl]
                        k.dma(SP, xv, xT_tile(s, t0, N), xb[sl], writes=[xb[sl]])
                        k.dma(SP, yna[:, sl], ynaT[s].rearrange("(c p) t -> p c t", p=128)[:, :, t0:t0 + N], ynb[sl], writes=[ynb[sl]])
                        k.dma(SP, ylr[:, sl], ylruT[s].rearrange("(c p) t -> p c t", p=128)[:, :, t0:t0 + N], ylb[sl], writes=[ylb[sl]])
                        k.dma(SP, yca[:, sl], ycaT[s].rearrange("(c p) t -> p c t", p=128)[:, :, t0:t0 + N], ycb[sl], writes=[ycb[sl]])
                        for q in range(4):
                            k.dma(SP, gt[:, sl, 6 * q:6 * q + 6, :], gtT[s].rearrange("(c p) t -> p c t", p=128)[:, 6 * q:6 * q + 6, t0:t0 + N],
                                  gtb[sl], writes=[gtb[sl]])
                        for oc in range(KC):
                            ba = oc % 2
                            pa = ps[:, ba, :]
                            pc = ps[:, 2 + ba, 0:N]
                            for h in range(4):
                                k.op(PE, lambda e, h=h, oc=oc: e.matmul(pa[:, 0:N], lhsT=wna[:, h, oc * 128:(oc + 1) * 128], rhs=yna[:, sl, h, :],
                                                                    start=(h == 0), stop=(h == 3)), reads=[ynb[sl]], writes=[pb[ba]])
                            for c in range(4):
                                k.op(PE, lambda e, c=c, oc=oc: e.matmul(pa[:, N:2 * N], lhsT=wlr[:, c, oc * 128:(oc + 1) * 128], rhs=ylr[:, sl, c, :],
                                                                    start=(c == 0), stop=(c == 3)), reads=[ylb[sl]], writes=[pb[ba]])
                            for c in range(4):
                                k.op(PE, lambda e, c=c, oc=oc: e.matmul(pc, lhsT=wca[:, c, oc * 128:(oc + 1) * 128], rhs=yca[:, sl, c, :],
                                                                    start=(c == 0), stop=(c == 3)), reads=[ycb[sl]], writes=[pb[2 + ba]])
                            k.op(DVE, lambda e, oc=oc: e.tensor_tensor(out=tmp[:, 0, :], in0=pa[:, 0:N], in1=gt[:, sl, oc, :], op=ALU.mult),
                                 reads=[pb[ba], gtb[sl]], writes=[tb[0]])
                            k.op(DVE, lambda e, oc=oc: e.tensor_tensor(out=tmp[:, 1, :], in0=pa[:, N:2 * N], in1=gt[:, sl, 8 + oc, :], op=ALU.mult),
                                 reads=[pb[ba], gtb[sl]], writes=[tb[1]])
                            k.op(DVE, lambda e, oc=oc: e.tensor_tensor(out=tmp[:, 2, :], in0=pc, in1=gt[:, sl, 16 + oc, :], op=ALU.mult),
                                 reads=[pb[2 + ba], gtb[sl]], writes=[tb[2]])
                            k.op(POOL, lambda e: e.tensor_tensor(out=tmp[:, 0, :], in0=tmp[:, 0, :], in1=tmp[:, 1, :], op=ALU.add),
                                 reads=[tb[0], tb[1]], writes=[tb[0]])
                            k.op(POOL, lambda e, oc=oc: e.tensor_tensor(out=mgb[:, oc, :], in0=tmp[:, 0, :], in1=tmp[:, 2, :], op=ALU.add),
                                 reads=[tb[0], tb[2]], writes=[mgbb[oc]])
                        for oc in range(KC):
                            pd = ps[:, 5 + oc % 2, 0:N]
                            for c in range(KC):
                                k.op(PE, lambda e, c=c, oc=oc: e.matmul(pd, lhsT=wo[:, c, oc * 128:(oc + 1) * 128], rhs=mgb[:, c, :],
                                                                    start=(c == 0), stop=(c == KC - 1)), reads=[mgbb[c]], writes=[pb[5 + oc % 2]])
                            k.op(ACT, lambda e, oc=oc: e.activation(out=yy[:, oc, :], in_=pd, func=AF.Copy), reads=[pb[5 + oc % 2]], writes=[yb])
                        postnorm_residual(yy[:], yb, xv, xb[sl], hs[:], hsb, vec[:, V_GMPOST:V_GMPOST + 8], ps[:, 4, 0:N], pb[4],
                                          rs[:, 0, :], rsb, False)
                        k.dma(POOL, xT_tile(s, t0, N), xv, xb[sl], reads=[xb[sl]])
                k.barrier()

        build.phases = dict(ffn=phase_ffn, mixin=phase_mixin, na=phase_na, lru=phase_lru, merge=phase_merge)
        if only == 'ffn1':
            phase_ffn(0, w_up1, w_dn1, V_G1PRE, V_G1POST, first=True, last=True)
        if only is not None and only.startswith('upto'):
            n = int(only[4:])
            plist = [lambda: phase_ffn(0, w_up1, w_dn1, V_G1PRE, V_G1POST, first=True, last=False), lambda: phase_mixin(0),
                     lambda: phase_na(0), lambda: phase_lru(0), lambda: phase_merge(0),
                     lambda: phase_ffn(0, w_up2, w_dn2, V_G2PRE, V_G2POST, first=False, last=True)]
            for f in plist[:n]:
                f()
        for l in range(L if only is None else 0):
            phase_ffn(l, w_up1, w_dn1, V_G1PRE, V_G1POST, first=(l == 0), last=False)
            phase_mixin(l)
            phase_na(l)
            phase_lru(l)
            phase_merge(l)
            phase_ffn(l, w_up2, w_dn2, V_G2PRE, V_G2POST, first=False, last=(l == L - 1))
    return nc


def _cols(v, nch):
    return np.ascontiguousarray(np.asarray(v, np.float32).reshape(nch, 128).T)


def host_prep(inputs, L):
    vec = np.zeros((L, 128, NVEC), np.float32)
    for l in range(L):
        for off, name in ((V_G1PRE, 'g_ffn1_pre'), (V_G1POST, 'g_ffn1_post'), (V_GMPRE, 'g_mix_pre'), (V_GMPOST, 'g_mix_post'),
                          (V_G2PRE, 'g_ffn2_pre'), (V_G2POST, 'g_ffn2_post'), (V_GMEM, 'g_mem')):
            vec[l, :, off:off + 8] = _cols(inputs[name][l], 8)
        vec[l, :, V_BG:V_BG + 24] = _cols(inputs['b_gate'][l], 24)
        for i in range(4):
            vec[l, :, V_CW + 4 * i:V_CW + 4 * i + 4] = _cols(inputs['conv_w'][l, i], 4)
        vec[l, :, V_CB:V_CB + 4] = _cols(inputs['conv_b'][l], 4)
        for d in range(2):
            vec[l, :, V_BA + 4 * d:V_BA + 4 * d + 4] = _cols(inputs['lru_ba'][l, d], 4)
            vec[l, :, V_BI + 4 * d:V_BI + 4 * d + 4] = _cols(inputs['lru_bi'][l, d], 4)
            vec[l, :, V_LAM + 4 * d:V_LAM + 4 * d + 4] = _cols(inputs['lru_lambda'][l, d], 4)
    kr2 = np.arange(2)[:, None, None, None]
    kc = np.arange(64)[None, :, None, None]
    sl = np.arange(16)[None, None, :, None]
    qc = np.arange(64)[None, None, None, :]
    dr = kr2 + 14 - sl + 0 * kc + 0 * qc
    dc = kc - qc + 15 + 0 * kr2 + 0 * sl
    ws = np.clip(qc - 8, 0, 48)
    colok = (kc >= ws) & (kc < ws + 16)
    ok = (dr >= 0) & (dr <= 14) & colok
    rpb = np.asarray(inputs['na_rpb'], np.float32)[:L]
    g = rpb[:, :, np.clip(dr, 0, 14), np.clip(dc, 0, 30)]
    rpbg = np.ascontiguousarray(g.reshape(L, H_NA, 128, 1024))
    cmask = np.ascontiguousarray(np.where(ok, 0.0, NEG).astype(np.float32).reshape(128, 1024))
    return vec, rpbg, cmask


WNAMES = ['w_ffn1_up', 'w_ffn1_down', 'w_ffn2_up', 'w_ffn2_down', 'w_in', 'w_gate', 'w_mem_kv',
          'w_branch_na', 'w_branch_lru', 'w_branch_ca', 'w_out', 'lru_wa', 'lru_wi']


def run(inputs, xs, mems, T, L, NS, ncores, only=None):
    vec, rpbg, cmask = host_prep(inputs, L)
    nc = build(T=T, L=L, NS=NS, only=only)
    common = {n: np.ascontiguousarray(np.asarray(inputs[n], np.float32)[:L]) for n in WNAMES}
    common.update(vecs=vec, rpbg=rpbg, cmask=cmask, cident=np.eye(128, dtype=np.float32))
    in_maps = []
    for c in range(ncores):
        m = dict(common)
        m['x0'] = np.ascontiguousarray(xs[c], dtype=np.float32)
        m['mem0'] = np.ascontiguousarray(mems[c], dtype=np.float32)
        in_maps.append(m)
    res = run_bass_kernel_spmd(nc, in_maps, core_ids=list(range(ncores)))
    return [r['yout'] for r in res.results]


def kernel(**inputs):
    xp = np.asarray(inputs['x_prompt'], np.float32)
    xs_ = np.asarray(inputs['x_sample'], np.float32)
    mp = np.asarray(inputs['mem_prompt'], np.float32)
    ms = np.asarray(inputs['mem_sample'], np.float32)
    xs = [np.stack([xp[c], xs_[c % 2]]) for c in range(8)]
    mems = [np.stack([mp[c], ms[c % 2]]) for c in range(8)]
    outs = run(inputs, xs, mems, T=8192, L=4, NS=2, ncores=8)
    y_prompt = np.stack([outs[c][0] for c in range(8)]).astype(np.float32)
    y_sample = np.stack([outs[c][1] for c in range(2)]).astype(np.float32)
    return (y_prompt, y_sample)
```

```python
import numpy as np
import concourse.bass as bass
import concourse.mybir as mybir
from concourse.bass_utils import run_bass_kernel_spmd

F32 = mybir.dt.float32
BF16 = mybir.dt.bfloat16
AF = mybir.ActivationFunctionType
ALU = mybir.AluOpType

D = 1024
KC = 8
FF = 2816
FC = 22
NTOK = 256
H_NA = 8
NMEM = 256
EPS = 1e-6
NEG = -1e30
NVEC = 124
V_G1PRE, V_G1POST, V_GMPRE, V_GMPOST, V_G2PRE, V_G2POST = 0, 8, 16, 24, 32, 40
V_BG = 48
V_CW = 72
V_CB = 88
V_BA = 92
V_BI = 100
V_LAM = 108
V_GMEM = 116


class Eng:
    def __init__(self, nc, name, eng, counts=True):
        self.nc, self.name, self.eng = nc, name, eng
        self.sem = nc.alloc_semaphore(name="s_" + name) if counts else None
        self.cnt = 0
        self.seen = {}

    def wait(self, tok):
        sem, val, key, ename = tok
        if ename == self.name and self.name == "pe":
            return
        if self.seen.get(key, 0) >= val:
            return
        self.seen[key] = val
        self.eng.wait_ge(sem, val)


class Buf:
    __slots__ = ("w", "r", "dsem", "name")

    def __init__(self, name=""):
        self.w = {}
        self.r = {}
        self.dsem = None
        self.name = name


class K:
    def __init__(self, nc):
        self.nc = nc
        self.pe = Eng(nc, "pe", nc.tensor)
        self.act = Eng(nc, "act", nc.scalar)
        self.dve = Eng(nc, "dve", nc.vector)
        self.pool = Eng(nc, "pool", nc.gpsimd)
        self.sp = Eng(nc, "sp", nc.sync, counts=False)
        self.engs = [self.pe, self.act, self.dve, self.pool, self.sp]
        self.dpool = []
        self.dused = []
        self.nd = 0
        self.phase_bufs = []

    def buf(self, name=""):
        b = Buf(name)
        self.phase_bufs.append(b)
        return b

    def bufs(self, n, name=""):
        return [self.buf(name + str(i)) for i in range(n)]

    def _deps(self, E, reads, writes):
        for b in reads:
            for key, tok in b.w.items():
                E.wait(tok)
        for b in writes:
            for key, tok in b.r.items():
                if tok[3] != E.name:
                    E.wait(tok)
            for key, tok in b.w.items():
                if tok[3] != E.name:
                    E.wait(tok)

    def _mark(self, tok, reads, writes):
        for b in reads:
            b.r[tok[2]] = tok
        for b in writes:
            b.w = {tok[2]: tok}
            b.r = {}

    def op(self, E, fn, reads=(), writes=()):
        self._deps(E, reads, writes)
        ins = fn(E.eng)
        E.cnt += 1
        ins.then_inc(E.sem, 1)
        tok = (E.sem, E.cnt, E.name, E.name)
        self._mark(tok, reads, writes)
        return tok

    def _dsem(self, b):
        if b.dsem is None:
            if self.dpool:
                b.dsem = self.dpool.pop()
            else:
                self.nd += 1
                b.dsem = [self.nc.alloc_semaphore(name="d%d" % self.nd), 0, "d%d" % self.nd]
            self.dused.append(b.dsem)
        return b.dsem

    def dma(self, E, out, in_, sb, reads=(), writes=()):
        self._deps(E, reads, writes)
        ds = self._dsem(sb)
        ds[1] += 16
        E.eng.dma_start(out=out, in_=in_).then_inc(ds[0], 16)
        tok = (ds[0], ds[1], ds[2], "dma")
        self._mark(tok, reads, writes)
        return tok

    def barrier(self):
        toks = []
        for F in self.engs:
            if F.sem is not None and F.cnt > 0:
                toks.append((F.sem, F.cnt, F.name, F.name + "_bar"))
        for ds in self.dused:
            if ds[1] > 0:
                toks.append((ds[0], ds[1], ds[2], "dma"))
        for E in self.engs:
            for t in toks:
                if t[2] == E.name:
                    continue
                E.wait(t)
        for ds in self.dused:
            self.dpool.append(ds)
        self.dused = []
        for b in self.phase_bufs:
            b.w, b.r, b.dsem = {}, {}, None
        self.phase_bufs = []


def na_rs(r, rows):
    return min(max(r - 4, 0), rows - 8)


def build(T=8192, L=4, NS=2, only=None):
    nc = bass.Bass("TRN2", target_bir_lowering=False)
    k = K(nc)
    PE, ACT, DVE, POOL, SP = k.pe, k.act, k.dve, k.pool, k.sp
    ROWS = T // 64
    NT = T // NTOK
    NG = ROWS // 8
    N = NTOK

    def din(name, shape, dt=F32):
        return nc.dram_tensor(name, list(shape), dt, kind="ExternalInput").ap()

    def dscr(name, shape, dt):
        return nc.dram_tensor(name, list(shape), dt, kind="Internal").ap()

    x0 = din("x0", [NS, T, D])
    mem0 = din("mem0", [NS, NMEM, D])
    vecs = din("vecs", [L, 128, NVEC])
    w_up1 = din("w_ffn1_up", [L, D, 2 * FF]); w_dn1 = din("w_ffn1_down", [L, FF, D])
    w_up2 = din("w_ffn2_up", [L, D, 2 * FF]); w_dn2 = din("w_ffn2_down", [L, FF, D])
    w_in = din("w_in", [L, D, 3072]); w_gate = din("w_gate", [L, D, 3072])
    w_kv = din("w_mem_kv", [L, D, 1024])
    w_bna = din("w_branch_na", [L, 512, D]); w_blru = din("w_branch_lru", [L, 512, D])
    w_bca = din("w_branch_ca", [L, 512, D]); w_out = din("w_out", [L, D, D])
    lru_wa = din("lru_wa", [L, 2, 8, 64, 64]); lru_wi = din("lru_wi", [L, 2, 8, 64, 64])
    rpbg = din("rpbg", [L, H_NA, 128, 1024])
    cmask = din("cmask", [128, 1024])
    cident = din("cident", [128, 128])
    yout = nc.dram_tensor("yout", [NS, T, D], F32, kind="ExternalOutput").ap()

    xT = dscr("xT", [NS, D, T], F32)
    qT = dscr("qT", [NS, 512, T], BF16); kT = dscr("kT", [NS, 512, T], BF16)
    Vt = dscr("Vt", [NS, T, 512], BF16)
    xlT = dscr("xlT", [NS, 512, T], F32); glT = dscr("glT", [NS, 512, T], F32)
    gtT = dscr("gtT", [NS, 3072, T], BF16)
    ycaT = dscr("ycaT", [NS, 512, T], BF16); ylruT = dscr("ylruT", [NS, 512, T], BF16)
    ynaT = dscr("ynaT", [NS, 512, T], BF16)

    cast_rr = [0]
    uctr = [0]

    def un(n):
        uctr[0] += 1
        return "%s_%d" % (n, uctr[0])

    def cast(out, in_, reads, writes):
        i = cast_rr[0] % 3
        cast_rr[0] += 1
        if i == 0:
            return k.op(ACT, lambda e: e.activation(out=out, in_=in_, func=AF.Copy), reads, writes)
        if i == 1:
            return k.op(DVE, lambda e: e.tensor_copy(out=out, in_=in_), reads, writes)
        return k.op(POOL, lambda e: e.tensor_copy(out=out, in_=in_), reads, writes)

    with nc.psum_tensor("ps", [128, 8, 512], F32) as ps, \
            nc.sbuf_tensor("vec", [128, NVEC], F32) as vec, \
            nc.sbuf_tensor("identf", [128, 128], F32) as identf, \
            nc.sbuf_tensor("identb", [128, 128], BF16) as identb, \
            nc.sbuf_tensor("onesm", [128, 128], BF16) as onesm, \
            nc.sbuf_tensor("onesb", [128, 128], BF16) as onesb, \
            nc.sbuf_tensor("zerob", [128, 128], BF16) as zerob:

        cb = k.buf("const")
        k.dma(SP, identf[:], cident[:, :], cb, writes=[cb])
        k.op(DVE, lambda e: e.tensor_copy(out=identb[:], in_=identf[:]), reads=[cb], writes=[cb])
        k.op(DVE, lambda e: e.memset(onesm[:], 1.0 / 1024.0), writes=[cb])
        k.op(DVE, lambda e: e.memset(onesb[:], 1.0), writes=[cb])
        k.op(DVE, lambda e: e.memset(zerob[:], 0.0), writes=[cb])
        k.barrier()

        def load_vec(l):
            b = k.buf("vec")
            k.dma(SP, vec[:], vecs[l], b, writes=[b])
            return b

        def load_weight(dst3, src2, stage, stage_bufs, rows=128):
            nk, ncol = dst3.shape[1], dst3.shape[2]
            SW = stage.shape[2]
            idx = load_weight.idx
            for kk in range(nk):
                for c0 in range(0, ncol, SW):
                    cw = min(SW, ncol - c0)
                    s = idx % len(stage_bufs)
                    idx += 1
                    sb = stage_bufs[s]
                    k.dma(SP, stage[0:rows, s, 0:cw], src2[kk * rows:(kk + 1) * rows, c0:c0 + cw], sb, writes=[sb])
                    cast(dst3[:, kk, c0:c0 + cw], stage[0:rows, s, 0:cw], [sb], [])
            load_weight.idx = idx
        load_weight.idx = 0

        def rmsnorm(xv, xb, hs, hsb, gcol, psn, psnb, rs, rsb, n):
            k.op(POOL, lambda e: e.tensor_tensor(out=hs, in0=xv, in1=xv, op=ALU.mult), reads=[xb], writes=[hsb])
            for c in range(KC):
                k.op(PE, lambda e, c=c: e.matmul(psn, lhsT=onesm[:, :], rhs=hs[:, c, :], start=(c == 0), stop=(c == KC - 1)),
                     reads=[hsb], writes=[psnb])
            k.op(ACT, lambda e: e.activation(out=rs, in_=psn, func=AF.Sqrt, bias=EPS, scale=1.0), reads=[psnb], writes=[rsb])
            k.op(DVE, lambda e: e.reciprocal(out=rs, in_=rs), reads=[rsb], writes=[rsb])
            for c in range(KC):
                k.op(DVE, lambda e, c=c: e.scalar_tensor_tensor(out=hs[:, c, :], in0=xv[:, c, :], scalar=gcol[:, c:c + 1], in1=rs,
                                                               op0=ALU.mult, op1=ALU.mult), reads=[xb, rsb], writes=[hsb])

        def postnorm_residual(y, yb, xv, xb, hs, hsb, gcol, psn, psnb, rs, rsb, half):
            k.op(POOL, lambda e: e.tensor_tensor(out=hs, in0=y, in1=y, op=ALU.mult), reads=[yb], writes=[hsb])
            for c in range(KC):
                k.op(PE, lambda e, c=c: e.matmul(psn, lhsT=onesm[:, :], rhs=hs[:, c, :], start=(c == 0), stop=(c == KC - 1)),
                     reads=[hsb], writes=[psnb])
            k.op(ACT, lambda e: e.activation(out=rs, in_=psn, func=AF.Sqrt, bias=EPS, scale=1.0), reads=[psnb], writes=[rsb])
            k.op(DVE, lambda e: e.reciprocal(out=rs, in_=rs), reads=[rsb], writes=[rsb])
            for c in range(KC):
                k.op(DVE, lambda e, c=c: e.scalar_tensor_tensor(out=y[:, c, :], in0=y[:, c, :], scalar=gcol[:, c:c + 1], in1=rs,
                                                               op0=ALU.mult, op1=ALU.mult), reads=[yb, rsb], writes=[yb])
                if half:
                    k.op(DVE, lambda e, c=c: e.scalar_tensor_tensor(out=xv[:, c, :], in0=y[:, c, :], scalar=0.5, in1=xv[:, c, :],
                                                                   op0=ALU.mult, op1=ALU.add), reads=[yb, xb], writes=[xb])
                else:
                    k.op(DVE, lambda e, c=c: e.tensor_tensor(out=xv[:, c, :], in0=y[:, c, :], in1=xv[:, c, :], op=ALU.add),
                         reads=[yb, xb], writes=[xb])

        def xT_tile(s, t0, n):
            return xT[s].rearrange("(c p) t -> p c t", p=128)[:, :, t0:t0 + n]

        def phase_ffn(l, w_up, w_dn, gpre, gpost, first, last):
            with nc.sbuf_tensor(un("wup"), [128, KC, 2 * FF], BF16) as wup, \
                    nc.sbuf_tensor(un("wdn"), [128, FC, D], BF16) as wdn, \
                    nc.sbuf_tensor(un("xx"), [128, 3, KC, N], F32) as xx, \
                    nc.sbuf_tensor(un("hs"), [128, 2, KC, N], BF16) as hs2, \
                    nc.sbuf_tensor(un("uu"), [128, FC, N], BF16) as uu, \
                    nc.sbuf_tensor(un("yy"), [128, KC, N], F32) as yy, \
                    nc.sbuf_tensor(un("rs"), [128, 2, N], F32) as rs, \
                    nc.sbuf_tensor(un("st"), [128, 2, N], F32) as st, \
                    nc.sbuf_tensor(un("tm"), [128, 2, D], F32) as tm:
                vb = load_vec(l)
                stage = xx[:].rearrange("p a c n -> p (a c n)").rearrange("p (s w) -> p s w", s=6)
                sbs = k.bufs(6, "stg")
                load_weight(wup[:], w_up[l], stage, sbs)
                load_weight(wdn[:], w_dn[l], stage, sbs)
                k.barrier()
                xb = k.bufs(3, "x"); hsb2 = k.bufs(2, "hs"); ub = k.bufs(FC, "u"); yb = k.buf("y")
                rsb = k.bufs(2, "rs"); stb = k.bufs(2, "st"); tmb = k.buf("tm")
                pab = k.bufs(2, "pab"); pdb = k.bufs(2, "pd"); pnb = k.bufs(2, "pn"); ptb = k.bufs(2, "pt")
                tiles = [(s, ti) for s in range(NS) for ti in range(NT)]
                NTI = len(tiles)

                def load(i):
                    s, ti = tiles[i]
                    t0 = ti * N
                    sl = i % 3
                    xv = xx[:, sl]
                    if first:
                        k.dma(SP, tm[:], x0[s, t0:t0 + N, :].rearrange("(a p) d -> p a d", p=128), tmb, writes=[tmb])
                        for bi in range(4):
                            pt = ps[:, 5 + bi % 2, :]
                            for cc in range(2):
                                for a_ in range(2):
                                    c = 2 * bi + cc
                                    k.op(PE, lambda e, c=c, a_=a_, cc=cc: e.transpose(
                                        out=pt[:, cc * 256 + a_ * 128: cc * 256 + a_ * 128 + 128],
                                        in_=tm[:, a_, c * 128:(c + 1) * 128], identity=identf[:]),
                                         reads=[tmb], writes=[ptb[bi % 2]])
                            dst = xv[:, 2 * bi:2 * bi + 2, :].rearrange("p c n -> p (c n)")
                            if bi % 2 == 0:
                                k.op(ACT, lambda e, dst=dst, pt=pt: e.activation(out=dst, in_=pt, func=AF.Copy),
                                     reads=[ptb[bi % 2]], writes=[xb[sl]])
                            else:
                                k.op(DVE, lambda e, dst=dst, pt=pt: e.tensor_copy(out=dst, in_=pt),
                                     reads=[ptb[bi % 2]], writes=[xb[sl]])
                    else:
                        k.dma(SP, xv, xT_tile(s, t0, N), xb[sl], writes=[xb[sl]])

                def prenorm(i):
                    rmsnorm(xx[:, i % 3], xb[i % 3], hs2[:, i % 2], hsb2[i % 2], vec[:, gpre:gpre + 8], ps[:, 4, 0:N], pnb[0],
                            rs[:, 0, :], rsb[0], N)

                def up(i, j0, j1):
                    hs = hs2[:, i % 2]
                    hsb = hsb2[i % 2]
                    for j in range(j0, j1):
                        pab_t = ps[:, j % 2, :]
                        for half in range(2):
                            for kk in range(KC):
                                k.op(PE, lambda e, kk=kk, half=half, j=j: e.matmul(
                                    pab_t[:, half * N:(half + 1) * N],
                                    lhsT=wup[:, kk, half * FF + j * 128: half * FF + (j + 1) * 128],
                                    rhs=hs[:, kk, :], start=(kk == 0), stop=(kk == KC - 1)),
                                     reads=[hsb], writes=[pab[j % 2]])
                        k.op(ACT, lambda e, j=j: e.activation(out=st[:, j % 2, :], in_=pab_t[:, 0:N], func=AF.Silu),
                             reads=[pab[j % 2]], writes=[stb[j % 2]])
                        k.op(DVE, lambda e, j=j: e.tensor_tensor(out=uu[:, j, :], in0=pab_t[:, N:2 * N], in1=st[:, j % 2, :], op=ALU.mult),
                             reads=[pab[j % 2], stb[j % 2]], writes=[ub[j]])

                def down(i):
                    for oc in range(KC):
                        pd = ps[:, 2 + oc % 2, 0:N]
                        for kk in range(FC):
                            k.op(PE, lambda e, kk=kk, oc=oc: e.matmul(pd, lhsT=wdn[:, kk, oc * 128:(oc + 1) * 128], rhs=uu[:, kk, :],
                                                                  start=(kk == 0), stop=(kk == FC - 1)),
                                 reads=[ub[kk]], writes=[pdb[oc % 2]])
                        if oc % 2 == 0:
                            k.op(ACT, lambda e, oc=oc: e.activation(out=yy[:, oc, :], in_=pd, func=AF.Copy), reads=[pdb[oc % 2]], writes=[yb])
                        else:
                            k.op(DVE, lambda e, oc=oc: e.tensor_copy(out=yy[:, oc, :], in_=pd), reads=[pdb[oc % 2]], writes=[yb])

                def post(i):
                    s, ti = tiles[i]
                    t0 = ti * N
                    sl = i % 3
                    xv = xx[:, sl]
                    postnorm_residual(yy[:], yb, xv, xb[sl], hs2[:, i % 2], hsb2[i % 2], vec[:, gpost:gpost + 8], ps[:, 7, 0:N], pnb[1],
                                      rs[:, 1, :], rsb[1], True)
                    if last:
                        for a_ in range(2):
                            for bi in range(2):
                                pt = ps[:, 5 + bi % 2, :]
                                for cc in range(4):
                                    c = 4 * bi + cc
                                    k.op(PE, lambda e, c=c, a_=a_, cc=cc: e.transpose(
                                        out=pt[:, cc * 128:(cc + 1) * 128], in_=xv[:, c, a_ * 128:(a_ + 1) * 128], identity=identf[:]),
                                         reads=[xb[sl]], writes=[ptb[bi % 2]])
                                dst = tm[:, a_, bi * 512:(bi + 1) * 512]
                                if bi % 2 == 0:
                                    k.op(ACT, lambda e, dst=dst, pt=pt: e.activation(out=dst, in_=pt, func=AF.Copy),
                                         reads=[ptb[bi % 2]], writes=[tmb])
                                else:
                                    k.op(DVE, lambda e, dst=dst, pt=pt: e.tensor_copy(out=dst, in_=pt),
                                         reads=[ptb[bi % 2]], writes=[tmb])
                        k.dma(POOL, yout[s, t0:t0 + N, :].rearrange("(a p) d -> p a d", p=128), tm[:], tmb, reads=[tmb])
                    else:
                        k.dma(POOL, xT_tile(s, t0, N), xv, xb[sl], reads=[xb[sl]])

                load(0)
                if NTI > 1:
                    load(1)
                prenorm(0)
                for i in range(NTI):
                    up(i, 0, 6)
                    if i >= 1:
                        post(i - 1)
                    if i + 2 < NTI:
                        load(i + 2)
                    up(i, 6, 12)
                    if i + 1 < NTI:
                        prenorm(i + 1)
                    up(i, 12, FC)
                    down(i)
                post(NTI - 1)
                k.barrier()

        def phase_mixin(l):
            with nc.sbuf_tensor(un("win"), [128, KC, 3072], BF16) as win, \
                    nc.sbuf_tensor(un("wg"), [128, KC, 3072], BF16) as wg, \
                    nc.sbuf_tensor(un("wkv"), [128, KC, 1024], BF16) as wkv, \
                    nc.sbuf_tensor(un("kca"), [128, NS, 4, NMEM], BF16) as kca, \
                    nc.sbuf_tensor(un("vca"), [128, NS, 2, 512], BF16) as vca, \
                    nc.sbuf_tensor(un("xx"), [128, 2, KC, N], F32) as xx, \
                    nc.sbuf_tensor(un("hs"), [128, 2, KC, N], BF16) as hs2, \
                    nc.sbuf_tensor(un("rs"), [128, 2, N], F32) as rs, \
                    nc.sbuf_tensor(un("qk"), [128, 8, N], BF16) as qk, \
                    nc.sbuf_tensor(un("vtok"), [128, 2, 512], BF16) as vtok, \
                    nc.sbuf_tensor(un("xl"), [128, 4, N], F32) as xl, \
                    nc.sbuf_tensor(un("gl"), [128, 4, N], F32) as gl, \
                    nc.sbuf_tensor(un("qca"), [128, 4, N], BF16) as qca, \
                    nc.sbuf_tensor(un("gt"), [128, 24, N], BF16) as gt, \
                    nc.sbuf_tensor(un("pT"), [128, 2, 2, N], BF16) as pT, \
                    nc.sbuf_tensor(un("yca"), [128, 4, N], BF16) as yca, \
                    nc.sbuf_tensor(un("tm"), [128, 2, D], F32) as tm:
                vb = load_vec(l)
                stage = xx[:].rearrange("p a c n -> p (a c n)").rearrange("p (s w) -> p s w", s=4)
                sbs = k.bufs(4, "stg")
                load_weight(win[:], w_in[l], stage, sbs)
                load_weight(wg[:], w_gate[l], stage, sbs)
                load_weight(wkv[:], w_kv[l], stage, sbs)
                k.barrier()
                xb = k.bufs(2, "x"); hsb = k.buf("hs"); rsb = k.bufs(2, "rs"); tmb = k.buf("tm")
                pb = k.bufs(8, "psb")
                hs = hs2[:, 0]
                for s in range(NS):
                    k.dma(SP, tm[:], mem0[s].rearrange("(a p) d -> p a d", p=128), tmb, writes=[tmb])
                    xv = xx[:, 0]
                    for bi in range(4):
                        pt = ps[:, 5 + bi % 2, :]
                        for cc in range(2):
                            for a in range(2):
                                c = 2 * bi + cc
                                k.op(PE, lambda e, c=c, a=a, cc=cc: e.transpose(
                                    out=pt[:, cc * 256 + a * 128: cc * 256 + a * 128 + 128],
                                    in_=tm[:, a, c * 128:(c + 1) * 128], identity=identf[:]),
                                     reads=[tmb], writes=[pb[5 + bi % 2]])
                        dst = xv[:, 2 * bi:2 * bi + 2, :].rearrange("p c n -> p (c n)")
                        k.op(DVE, lambda e, dst=dst, pt=pt: e.tensor_copy(out=dst, in_=pt), reads=[pb[5 + bi % 2]], writes=[xb[0]])
                    rmsnorm(xv, xb[0], hs[:], hsb, vec[:, V_GMEM:V_GMEM + 8], ps[:, 4, 0:N], pb[4], rs[:, 0, :], rsb[0], N)
                    for h in range(4):
                        pp = ps[:, h % 2, 0:NMEM]
                        for c in range(KC):
                            k.op(PE, lambda e, c=c, h=h: e.matmul(pp, lhsT=wkv[:, c, h * 128:(h + 1) * 128], rhs=hs[:, c, :],
                                                              start=(c == 0), stop=(c == KC - 1)), reads=[hsb], writes=[pb[h % 2]])
                        k.op(DVE, lambda e, h=h: e.tensor_copy(out=kca[:, s, h, :], in_=pp), reads=[pb[h % 2]], writes=[tmb])
                    for mc in range(2):
                        pp = ps[:, 2 + mc, :]
                        for c in range(KC):
                            k.op(PE, lambda e, c=c, mc=mc: e.matmul(pp, lhsT=hs[:, c, mc * 128:(mc + 1) * 128], rhs=wkv[:, c, 512:1024],
                                                                start=(c == 0), stop=(c == KC - 1)), reads=[hsb], writes=[pb[2 + mc]])
                        k.op(ACT, lambda e, mc=mc: e.activation(out=vca[:, s, mc, :], in_=pp, func=AF.Copy), reads=[pb[2 + mc]], writes=[tmb])
                k.barrier()
                xb = k.bufs(2, "x"); hsb = k.buf("hs"); rsb = k.bufs(2, "rs")
                pb = k.bufs(8, "psb")
                qkb = k.buf("qk"); vtb = k.buf("vt"); xlb = k.buf("xl"); glb = k.buf("gl"); qcb = k.bufs(4, "qca")
                gtb = k.bufs(4, "gt"); pTb = k.bufs(2, "pT"); ycb = k.buf("yca"); recb = k.buf("rec")
                it = 0
                bank = [0]

                def nb():
                    b = bank[0] % 4
                    bank[0] += 1
                    return b
                hsb2 = k.bufs(2, "hs")
                tiles = [(s_, ti_) for s_ in range(NS) for ti_ in range(NT)]

                def prenorm(i):
                    s_, ti_ = tiles[i]
                    k.dma(SP, xx[:, i % 2], xT_tile(s_, ti_ * N, N), xb[i % 2], writes=[xb[i % 2]])
                    rmsnorm(xx[:, i % 2], xb[i % 2], hs2[:, i % 2], hsb2[i % 2], vec[:, V_GMPRE:V_GMPRE + 8], ps[:, 4, 0:N], pb[4],
                            rs[:, 0, :], rsb[0], N)
                prenorm(0)
                for s in range(NS):
                    for ti in range(NT):
                        t0 = ti * N
                        sl = it % 2
                        hs = hs2[:, it % 2]
                        hsb = hsb2[it % 2]
                        it += 1
                        for oc in list(range(0, 8)) + list(range(12, 24)):
                            b = nb()
                            pp = ps[:, b, 0:N]
                            for c in range(KC):
                                k.op(PE, lambda e, c=c, oc=oc: e.matmul(pp, lhsT=win[:, c, oc * 128:(oc + 1) * 128], rhs=hs[:, c, :],
                                                                    start=(c == 0), stop=(c == KC - 1)), reads=[hsb], writes=[pb[b]])
                            if oc < 4:
                                k.op(ACT, lambda e, oc=oc: e.activation(out=qk[:, oc, :], in_=pp, func=AF.Copy, scale=0.125),
                                     reads=[pb[b]], writes=[qkb])
                            elif oc < 8:
                                k.op(DVE, lambda e, oc=oc: e.tensor_copy(out=qk[:, oc, :], in_=pp), reads=[pb[b]], writes=[qkb])
                            elif oc < 16:
                                k.op(ACT, lambda e, oc=oc: e.activation(out=xl[:, oc - 12, :], in_=pp, func=AF.Copy), reads=[pb[b]], writes=[xlb])
                            elif oc < 20:
                                k.op(ACT, lambda e, oc=oc: e.activation(out=gl[:, oc - 16, :], in_=pp, func=AF.Gelu), reads=[pb[b]], writes=[glb])
                            else:
                                k.op(DVE, lambda e, oc=oc: e.tensor_scalar(out=qca[:, oc - 20, :], in0=pp, scalar1=float(128 ** -0.5), scalar2=None,
                                                                       op0=ALU.mult), reads=[pb[b]], writes=[qcb[oc - 20]])
                        k.dma(POOL, qT[s].rearrange("(c p) t -> p c t", p=128)[:, :, t0:t0 + N], qk[:, 0:4, :], qkb, reads=[qkb])
                        k.dma(POOL, kT[s].rearrange("(c p) t -> p c t", p=128)[:, :, t0:t0 + N], qk[:, 4:8, :], qkb, reads=[qkb])
                        k.dma(POOL, xlT[s].rearrange("(c p) t -> p c t", p=128)[:, :, t0:t0 + N], xl[:], xlb, reads=[xlb])
                        k.dma(POOL, glT[s].rearrange("(c p) t -> p c t", p=128)[:, :, t0:t0 + N], gl[:], glb, reads=[glb])
                        for a in range(2):
                            b = nb()
                            pp = ps[:, b, :]
                            for c in range(KC):
                                k.op(PE, lambda e, c=c, a=a: e.matmul(pp, lhsT=hs[:, c, a * 128:(a + 1) * 128], rhs=win[:, c, 1024:1536],
                                                                  start=(c == 0), stop=(c == KC - 1)), reads=[hsb], writes=[pb[b]])
                            k.op(DVE, lambda e, a=a: e.tensor_copy(out=vtok[:, a, :], in_=pp), reads=[pb[b]], writes=[vtb])
                        k.dma(POOL, Vt[s, t0:t0 + N, :].rearrange("(a p) f -> p a f", p=128), vtok[:], vtb, reads=[vtb])
                        for h in range(4):
                            for mc in range(2):
                                b = nb()
                                pp = ps[:, b, 0:N]
                                k.op(PE, lambda e, h=h, mc=mc: e.matmul(pp, lhsT=kca[:, s, h, mc * 128:(mc + 1) * 128], rhs=qca[:, h, :],
                                                                    start=True, stop=True), reads=[qcb[h]], writes=[pb[b]])
                                k.op(ACT, lambda e, h=h, mc=mc: e.activation(out=pT[:, h % 2, mc, :], in_=pp, func=AF.Exp),
                                     reads=[pb[b]], writes=[pTb[h % 2]])
                            b = nb()
                            po = ps[:, b, :]
                            for mc in range(2):
                                k.op(PE, lambda e, h=h, mc=mc: e.matmul(po[:, 0:N], lhsT=vca[:, s, mc, h * 128:(h + 1) * 128], rhs=pT[:, h % 2, mc, :],
                                                                    start=(mc == 0), stop=(mc == 1)), reads=[pTb[h % 2]], writes=[pb[b]])
                            for mc in range(2):
                                k.op(PE, lambda e, h=h, mc=mc: e.matmul(po[:, N:2 * N], lhsT=onesb[:, :], rhs=pT[:, h % 2, mc, :],
                                                                    start=(mc == 0), stop=(mc == 1)), reads=[pTb[h % 2]], writes=[pb[b]])
                            k.op(DVE, lambda e: e.reciprocal(out=rs[:, 1, :], in_=po[:, N:2 * N]), reads=[pb[b]], writes=[recb])
                            k.op(DVE, lambda e, h=h: e.tensor_tensor(out=yca[:, h, :], in0=po[:, 0:N], in1=rs[:, 1, :], op=ALU.mult),
                                 reads=[pb[b], recb], writes=[ycb])
                        k.dma(POOL, ycaT[s].rearrange("(c p) t -> p c t", p=128)[:, :, t0:t0 + N], yca[:], ycb, reads=[ycb])
                        if it < len(tiles):
                            prenorm(it)
                        for oc in range(24):
                            b = nb()
                            pp = ps[:, b, 0:N]
                            for c in range(KC):
                                k.op(PE, lambda e, c=c, oc=oc: e.matmul(pp, lhsT=wg[:, c, oc * 128:(oc + 1) * 128], rhs=hs[:, c, :],
                                                                    start=(c == 0), stop=(c == KC - 1)), reads=[hsb], writes=[pb[b]])
                            k.op(ACT, lambda e, oc=oc: e.activation(out=gt[:, oc, :], in_=pp, func=AF.Sigmoid, bias=vec[:, V_BG + oc:V_BG + oc + 1],
                                                                 scale=1.0), reads=[pb[b]], writes=[gtb[oc // 6]])
                            if oc % 6 == 5:
                                q = oc // 6
                                k.dma(POOL, gtT[s].rearrange("(c p) t -> p c t", p=128)[:, 6 * q:6 * q + 6, t0:t0 + N], gt[:, 6 * q:6 * q + 6, :],
                                      gtb[q], reads=[gtb[q]])
                k.barrier()

        def phase_na(l):
            with nc.sbuf_tensor(un("qe"), [128, 2, 4, 512], BF16) as qe, \
                    nc.sbuf_tensor(un("qo"), [128, 2, 4, 512], BF16) as qo, \
                    nc.sbuf_tensor(un("kg"), [128, 2, 4, 1024], BF16) as kg, \
                    nc.sbuf_tensor(un("ve"), [128, 2, 8, 512], BF16) as ve, \
                    nc.sbuf_tensor(un("vo"), [128, 2, 8, 512], BF16) as vo, \
                    nc.sbuf_tensor(un("t2"), [128, H_NA, 1024], BF16) as t2, \
                    nc.sbuf_tensor(un("msk"), [128, 1024], F32) as msk, \
                    nc.sbuf_tensor(un("stg"), [128, 2, 1024], F32) as stg, \
                    nc.sbuf_tensor(un("pT"), [128, 4, 512], BF16) as pT, \
                    nc.sbuf_tensor(un("rec"), [128, 2, 512], F32) as rec, \
                    nc.sbuf_tensor(un("onesE"), [128, 128], BF16) as onesE, \
                    nc.sbuf_tensor(un("onesO"), [128, 128], BF16) as onesO, \
                    nc.sbuf_tensor(un("yna"), [128, 2, 4, 512], BF16) as yna:
                mb = k.buf("msk"); sbs = k.bufs(2, "stg"); t2b = k.buf("t2")
                k.dma(SP, msk[:], cmask[:, :], mb, writes=[mb])
                for h in range(H_NA):
                    sb = sbs[h % 2]
                    k.dma(SP, stg[:, h % 2, :], rpbg[l, h], sb, writes=[sb])
                    k.op(DVE, lambda e, h=h: e.tensor_tensor(out=t2[:, h, :], in0=stg[:, h % 2, :], in1=msk[:], op=ALU.add),
                         reads=[sb, mb], writes=[t2b])
                zb = k.buf("z")
                k.op(POOL, lambda e: e.memset(qe[:], 0.0), writes=[zb])
                k.op(POOL, lambda e: e.memset(qo[:], 0.0), writes=[zb])
                k.op(POOL, lambda e: e.memset(ve[:], 0.0), writes=[zb])
                k.op(POOL, lambda e: e.memset(vo[:], 0.0), writes=[zb])
                k.op(DVE, lambda e: e.memset(onesE[:], 0.0), writes=[zb])
                k.op(DVE, lambda e: e.memset(onesO[:], 0.0), writes=[zb])
                k.op(DVE, lambda e: e.memset(onesE[0:64, :], 1.0), writes=[zb])
                k.op(DVE, lambda e: e.memset(onesO[64:128, :], 1.0), writes=[zb])
                k.barrier()
                qb = k.bufs(2, "qg"); kb = k.bufs(2, "kg"); vb_ = k.bufs(2, "vg")
                pTb = k.bufs(3, "pT"); recb = k.bufs(2, "rec"); ynb = k.bufs(2, "yna")
                psb = k.bufs(2, "pst"); pob = k.bufs(2, "po"); pqb = k.bufs(2, "pq")
                it = 0
                qv = lambda s: qT[s].rearrange("(c p) t -> p c t", p=128)
                STB = [0, 1, 6]
                psb = k.bufs(3, "pst")
                pTb = k.bufs(4, "pT")
                work = []
                ih = 0
                ipair = 0
                for s in range(NS):
                    for g in range(NG):
                        sl = it % 2
                        it += 1
                        r0 = 8 * g
                        kr_lo = na_rs(r0, ROWS)
                        kr_hi = na_rs(r0 + 7, ROWS) + 7
                        p_lo, p_hi = kr_lo // 2, kr_hi // 2
                        npair = p_hi - p_lo + 1

                        def loads(s=s, sl=sl, r0=r0, p_lo=p_lo, p_hi=p_hi, npair=npair):
                            k.dma(SP, qe[0:64, sl], qv(s)[0:64, :, r0 * 64:r0 * 64 + 512], qb[sl], writes=[qb[sl]])
                            k.dma(SP, qo[64:128, sl], qv(s)[64:128, :, r0 * 64:r0 * 64 + 512], qb[sl], writes=[qb[sl]])
                            k.dma(SP, kg[:, sl, :, 0:npair * 128], kT[s].rearrange("(c p) t -> p c t", p=128)[:, :, p_lo * 128:(p_hi + 1) * 128],
                                  kb[sl], writes=[kb[sl]])
                            vv = Vt[s, p_lo * 128:(p_hi + 1) * 128, :].rearrange("(m p) f -> p m f", p=128)
                            k.dma(SP, ve[0:64, sl, 0:npair, :], vv[0:64], vb_[sl], writes=[vb_[sl]])
                            k.dma(SP, vo[64:128, sl, 0:npair, :], vv[64:128], vb_[sl], writes=[vb_[sl]])
                        first_of_group = True
                        for h in range(H_NA):
                            c = h // 2
                            pbase = (h % 2) * 64
                            qz = qe if h % 2 == 0 else qo
                            hs_ = ih % 2
                            ih += 1
                            po = ps[:, 2 + hs_, :]
                            pq = ps[:, 4 + hs_, :]
                            items = []
                            for j in range(npair):
                                mp = p_lo + j
                                vr = []
                                for half in range(2):
                                    m = 2 * mp + half
                                    rr_ = [r for r in range(r0, r0 + 8) if na_rs(r, ROWS) <= m <= na_rs(r, ROWS) + 7]
                                    vr.append((rr_[0], rr_[-1]) if rr_ else None)
                                los = [v[0] for v in vr if v]
                                his = [v[1] for v in vr if v]
                                if not los:
                                    continue
                                items.append((j, mp, min(los), max(his), vr))
                            for ii, (j, mp, ra, rb, vr) in enumerate(items):
                                firstitem = ii == 0
                                lastitem = ii == len(items) - 1
                                ncol = (rb - ra + 1) * 64
                                c0 = (ra - r0) * 64
                                s0 = ra - 2 * mp + 7
                                assert 0 <= s0 and s0 + (rb - ra) < 16, (s0, ra, rb, mp)
                                sbi = ipair % 3
                                pi = ipair % 4
                                ipair += 1
                                pst = ps[:, STB[sbi], 0:ncol]

                                def stageA(ld=(loads if (first_of_group and firstitem) else None), firstitem=firstitem, po=po, pq=pq, hs_=hs_, sl=sl,
                                           pst=pst, j=j, c0=c0, ncol=ncol, s0=s0, sbi=sbi, pi=pi, qz=qz, c=c, h=h):
                                    if ld is not None:
                                        ld()
                                    if firstitem:
                                        k.op(PE, lambda e: e.matmul(po, lhsT=zerob[:, :], rhs=kg[:, sl, 0, 0:512], start=True, stop=False),
                                             reads=[kb[sl]], writes=[pob[hs_]])
                                        k.op(PE, lambda e: e.matmul(pq, lhsT=zerob[:, :], rhs=kg[:, sl, 0, 0:512], start=True, stop=False),
                                             reads=[kb[sl]], writes=[pqb[hs_]])
                                    k.op(PE, lambda e: e.matmul(pst, lhsT=kg[:, sl, c, j * 128:(j + 1) * 128],
                                                                rhs=qz[:, sl, c, c0:c0 + ncol], start=True, stop=False),
                                         reads=[kb[sl], qb[sl]], writes=[psb[sbi]])
                                    k.op(PE, lambda e: e.matmul(pst, lhsT=identb[:, :], rhs=t2[:, h, s0 * 64:s0 * 64 + ncol], start=False, stop=True),
                                         reads=[t2b], writes=[psb[sbi]])
                                    k.op(ACT, lambda e: e.activation(out=pT[:, pi, 0:ncol], in_=pst, func=AF.Exp),
                                         reads=[psb[sbi]], writes=[pTb[pi]])

                                def stageB(lastitem=lastitem, po=po, pq=pq, hs_=hs_, sl=sl, j=j, pi=pi, c=c, h=h, vr=vr, ra=ra, r0=r0, pbase=pbase,
                                           lastgh=(lastitem and h == H_NA - 1), s=s):
                                    nhalf = [hf for hf in range(2) if vr[hf]]
                                    for hi_, half in enumerate(nhalf):
                                        va, vb2 = vr[half]
                                        cl0 = (va - r0) * 64
                                        pc0 = (va - ra) * 64
                                        nv = (vb2 - va + 1) * 64
                                        fin = lastitem and hi_ == len(nhalf) - 1
                                        vz = ve if half == 0 else vo
                                        oz = onesE if half == 0 else onesO
                                        k.op(PE, lambda e: e.matmul(po[:, cl0:cl0 + nv], lhsT=vz[:, sl, j, c * 128:(c + 1) * 128],
                                                                    rhs=pT[:, pi, pc0:pc0 + nv], start=False, stop=fin),
                                             reads=[pTb[pi], vb_[sl]], writes=[pob[hs_]])
                                        k.op(PE, lambda e: e.matmul(pq[:, cl0:cl0 + nv], lhsT=oz[:, :],
                                                                    rhs=pT[:, pi, pc0:pc0 + nv], start=False, stop=fin),
                                             reads=[pTb[pi]], writes=[pqb[hs_]])
                                    if lastitem:
                                        k.op(DVE, lambda e: e.reciprocal(out=rec[pbase:pbase + 64, hs_, :], in_=pq[pbase:pbase + 64, :]),
                                             reads=[pqb[hs_]], writes=[recb[hs_]])
                                        k.op(DVE, lambda e: e.tensor_tensor(out=yna[pbase:pbase + 64, sl, c, :], in0=po[pbase:pbase + 64, :],
                                                                            in1=rec[pbase:pbase + 64, hs_, :], op=ALU.mult),
                                             reads=[pob[hs_], recb[hs_]], writes=[ynb[sl]])
                                    if lastgh:
                                        k.dma(POOL, ynaT[s].rearrange("(c p) t -> p c t", p=128)[:, :, r0 * 64:r0 * 64 + 512], yna[:, sl], ynb[sl],
                                              reads=[ynb[sl]])
                                work.append((stageA, stageB))
                            first_of_group = False
                LOOK = 2
                for idx in range(len(work) + LOOK):
                    if idx < len(work):
                        work[idx][0]()
                    if idx >= LOOK:
                        work[idx - LOOK][1]()
                k.barrier()

        def phase_lru(l):
            PC = min(1024, T)
            NP = T // PC
            with nc.sbuf_tensor(un("xl"), [128, T], F32) as xl, \
                    nc.sbuf_tensor(un("xc"), [128, T], F32) as xc, \
                    nc.sbuf_tensor(un("xcb"), [128, T], BF16) as xcb, \
                    nc.sbuf_tensor(un("wst"), [128, 16, 128], F32) as wst, \
                    nc.sbuf_tensor(un("wbd"), [128, 16, 128], BF16) as wbd, \
                    nc.sbuf_tensor(un("sm"), [128, 64], F32) as sm, \
                    nc.sbuf_tensor(un("rr"), [128, 2, PC], F32) as rr2, \
                    nc.sbuf_tensor(un("ii"), [128, 2, PC], F32) as ii2, \
                    nc.sbuf_tensor(un("aa"), [128, 2, PC], F32) as aa2, \
                    nc.sbuf_tensor(un("hb"), [128, 2, PC], F32) as hb, \
                    nc.sbuf_tensor(un("glp"), [128, 2, PC], F32) as glp, \
                    nc.sbuf_tensor(un("hsum"), [128, PC], F32) as hsum, \
                    nc.sbuf_tensor(un("yl"), [128, 2, PC], BF16) as yl:
                vb = load_vec(l)
                wb_ = k.buf("wbd")
                k.op(DVE, lambda e: e.memset(wst[:], 0.0), writes=[wb_])
                for d in range(2):
                    for gi, wsrc in enumerate((lru_wa, lru_wi)):
                        for jb in range(2):
                            dst = wst[jb * 64:(jb + 1) * 64, :, jb * 64:(jb + 1) * 64].rearrange("p (c x) m -> p c x m", x=4)[:, :, d * 2 + gi, :]
                            src = wsrc[l, d].rearrange("(c j) k m -> j k c m", j=2)[jb]
                            k.dma(SP, dst, src, wb_, writes=[wb_])
                k.op(DVE, lambda e: e.tensor_copy(out=wbd[:], in_=wst[:]), reads=[wb_], writes=[wb_])
                smb = k.buf("sm")
                lam = vec[:, V_LAM:V_LAM + 8]
                e_ = sm[:, 16:24]; es = sm[:, 24:32]; pp_ = sm[:, 32:40]; lnp = sm[:, 40:48]; mm_ = sm[:, 48:56]
                k.op(ACT, lambda e: e.activation(out=e_, in_=lam, func=AF.Exp, scale=-1.0), reads=[vb], writes=[smb])
                k.op(ACT, lambda e: e.activation(out=lnp, in_=e_, func=AF.Ln, bias=1.0, scale=1.0), reads=[smb], writes=[smb])
                k.op(DVE, lambda e: e.tensor_scalar(out=es, in0=e_, scalar1=0.1, scalar2=None, op0=ALU.min), reads=[smb], writes=[smb])
                k.op(DVE, lambda e: e.tensor_scalar(out=pp_, in0=es, scalar1=-1.0 / 6.0, scalar2=0.2, op0=ALU.mult, op1=ALU.add), reads=[smb], writes=[smb])
                for cf in (-0.25, 1.0 / 3.0, -0.5, 1.0):
                    k.op(DVE, lambda e: e.tensor_tensor(out=pp_, in0=pp_, in1=es, op=ALU.mult), reads=[smb], writes=[smb])
                    k.op(DVE, lambda e, cf=cf: e.tensor_scalar(out=pp_, in0=pp_, scalar1=float(cf), scalar2=None, op0=ALU.add), reads=[smb], writes=[smb])
                k.op(DVE, lambda e: e.tensor_tensor(out=pp_, in0=pp_, in1=es, op=ALU.mult), reads=[smb], writes=[smb])
                k.op(DVE, lambda e: e.tensor_scalar(out=mm_, in0=e_, scalar1=0.1, scalar2=None, op0=ALU.is_lt), reads=[smb], writes=[smb])
                k.op(DVE, lambda e: e.tensor_tensor(out=pp_, in0=pp_, in1=lnp, op=ALU.subtract), reads=[smb], writes=[smb])
                k.op(DVE, lambda e: e.tensor_tensor(out=pp_, in0=pp_, in1=mm_, op=ALU.mult), reads=[smb], writes=[smb])
                k.op(DVE, lambda e: e.tensor_tensor(out=pp_, in0=pp_, in1=lnp, op=ALU.add), reads=[smb], writes=[smb])
                k.op(DVE, lambda e: e.tensor_scalar(out=sm[:, 0:8], in0=pp_, scalar1=-8.0, scalar2=None, op0=ALU.mult), reads=[smb], writes=[smb])
                k.op(DVE, lambda e: e.tensor_scalar(out=sm[:, 8:16], in0=pp_, scalar1=-16.0, scalar2=None, op0=ALU.mult), reads=[smb], writes=[smb])
                k.barrier()
                xlb = k.buf("xl"); xcbuf = k.buf("xc"); xcbb = k.buf("xcb")
                rrb2 = k.bufs(2, "rr"); iib2 = k.bufs(2, "ii"); aab2 = k.bufs(2, "aa"); pcn = 0; hbb = k.bufs(2, "hb"); glb = k.bufs(2, "gl"); ylb = k.bufs(2, "yl")
                prb = k.bufs(2, "pr"); pib = k.bufs(2, "pi"); hsb_ = k.buf("hsum")
                ip = 0
                for s in range(NS):
                    for c in range(4):
                        for q in range(NP):
                            k.dma(SP, xl[:, q * PC:(q + 1) * PC], xlT[s, c * 128:(c + 1) * 128, q * PC:(q + 1) * PC], xlb, writes=[xlb])
                        cw = lambda i: vec[:, V_CW + i * 4 + c: V_CW + i * 4 + c + 1]
                        k.op(DVE, lambda e: e.tensor_scalar(out=xc[:], in0=xl[:], scalar1=cw(2), scalar2=vec[:, V_CB + c:V_CB + c + 1],
                                                            op0=ALU.mult, op1=ALU.add), reads=[xlb], writes=[xcbuf])
                        for i in (0, 1, 3):
                            o = i - 2
                            d0, d1 = max(0, -o), T - max(0, o)
                            k.op(DVE, lambda e, i=i, o=o, d0=d0, d1=d1: e.scalar_tensor_tensor(
                                out=xc[:, d0:d1], in0=xl[:, d0 + o:d1 + o], scalar=cw(i), in1=xc[:, d0:d1], op0=ALU.mult, op1=ALU.add),
                                 reads=[xlb, xcbuf], writes=[xcbuf])
                        k.op(POOL, lambda e: e.tensor_copy(out=xcb[:], in_=xc[:]), reads=[xcbuf], writes=[xcbb])
                        hf = xl
                        for d in range(2):
                            order = range(NP) if d == 0 else range(NP - 1, -1, -1)
                            prev = None
                            for q in order:
                                t0 = q * PC
                                pp = pcn % 2
                                pcn += 1
                                rr = rr2[:, pp]; ii_ = ii2[:, pp]; aa = aa2[:, pp]
                                rrb = rrb2[pp]; iib = iib2[pp]; aab = aab2[pp]
                                for sub in range(PC // 512):
                                    pr = ps[:, 0 + sub % 2, :]
                                    pi_ = ps[:, 2 + sub % 2, :]
                                    tt = t0 + sub * 512
                                    k.op(PE, lambda e, pr=pr, tt=tt: e.matmul(pr, lhsT=wbd[:, c * 4 + d * 2 + 0, :], rhs=xcb[:, tt:tt + 512], start=True, stop=True),
                                         reads=[xcbb], writes=[prb[sub % 2]])
                                    k.op(PE, lambda e, pi_=pi_, tt=tt: e.matmul(pi_, lhsT=wbd[:, c * 4 + d * 2 + 1, :], rhs=xcb[:, tt:tt + 512], start=True, stop=True),
                                         reads=[xcbb], writes=[pib[sub % 2]])
                                    k.op(ACT, lambda e, pr=pr, sub=sub: e.activation(out=rr[:, sub * 512:(sub + 1) * 512], in_=pr, func=AF.Sigmoid,
                                                                                 bias=vec[:, V_BA + d * 4 + c:V_BA + d * 4 + c + 1], scale=1.0),
                                         reads=[prb[sub % 2]], writes=[rrb])
                                    k.op(ACT, lambda e, pi_=pi_, sub=sub: e.activation(out=ii_[:, sub * 512:(sub + 1) * 512], in_=pi_, func=AF.Sigmoid,
                                                                                   bias=vec[:, V_BI + d * 4 + c:V_BI + d * 4 + c + 1], scale=1.0),
                                         reads=[pib[sub % 2]], writes=[iib])
                                k.op(ACT, lambda e: e.activation(out=aa[:], in_=rr[:], func=AF.Exp, scale=sm[:, d * 4 + c:d * 4 + c + 1]),
                                     reads=[rrb], writes=[aab])
                                k.op(ACT, lambda e: e.activation(out=rr[:], in_=rr[:], func=AF.Exp, scale=sm[:, 8 + d * 4 + c:8 + d * 4 + c + 1]),
                                     reads=[rrb], writes=[rrb])
                                k.op(ACT, lambda e: e.activation(out=rr[:], in_=rr[:], func=AF.Sqrt, bias=1.0, scale=-1.0), reads=[rrb], writes=[rrb])
                                k.op(DVE, lambda e, t0=t0: e.tensor_tensor(out=ii_[:], in0=ii_[:], in1=xc[:, t0:t0 + PC], op=ALU.mult),
                                     reads=[iib, xcbuf], writes=[iib])
                                k.op(DVE, lambda e: e.tensor_tensor(out=ii_[:], in0=ii_[:], in1=rr[:], op=ALU.mult), reads=[iib, rrb], writes=[iib])
                                if d == 0:
                                    init = 0.0 if prev is None else hf[:, t0 - 1:t0]
                                    k.op(DVE, lambda e, t0=t0, init=init: e.tensor_tensor_scan(out=hf[:, t0:t0 + PC], data0=aa[:], data1=ii_[:], initial=init,
                                                                                             op0=ALU.mult, op1=ALU.add), reads=[aab, iib, xlb], writes=[xlb])
                                else:
                                    sl = ip % 2
                                    ip += 1
                                    k.dma(SP, glp[:, sl, :], glT[s, c * 128:(c + 1) * 128, t0:t0 + PC], glb[sl], writes=[glb[sl]])
                                    init = 0.0 if prev is None else hb[:, prev, 0:1]
                                    rdeps = [aab, iib] + ([hbb[prev]] if prev is not None else [])
                                    k.op(DVE, lambda e, sl=sl, init=init: e.tensor_tensor_scan(out=hb[:, sl, ::-1], data0=aa[:, ::-1], data1=ii_[:, ::-1], initial=init,
                                                                                             op0=ALU.mult, op1=ALU.add), reads=rdeps, writes=[hbb[sl]])
                                    k.op(POOL, lambda e, sl=sl, t0=t0: e.tensor_tensor(out=hsum[:], in0=hb[:, sl, :], in1=hf[:, t0:t0 + PC], op=ALU.add),
                                         reads=[hbb[sl], xlb], writes=[hsb_])
                                    k.op(POOL, lambda e, sl=sl: e.tensor_tensor(out=yl[:, sl, :], in0=hsum[:], in1=glp[:, sl, :], op=ALU.mult),
                                         reads=[hsb_, glb[sl]], writes=[ylb[sl]])
                                    k.dma(POOL, ylruT[s, c * 128:(c + 1) * 128, t0:t0 + PC], yl[:, sl, :], ylb[sl], reads=[ylb[sl]])
                                    prev = sl
                                if d == 0:
                                    prev = 0
                k.barrier()

        def phase_merge(l):
            with nc.sbuf_tensor(un("wna"), [128, 4, D], BF16) as wna, \
                    nc.sbuf_tensor(un("wlr"), [128, 4, D], BF16) as wlr, \
                    nc.sbuf_tensor(un("wca"), [128, 4, D], BF16) as wca, \
                    nc.sbuf_tensor(un("wo"), [128, KC, D], BF16) as wo, \
                    nc.sbuf_tensor(un("xx"), [128, 2, KC, N], F32) as xx, \
                    nc.sbuf_tensor(un("yna"), [128, 2, 4, N], BF16) as yna, \
                    nc.sbuf_tensor(un("ylr"), [128, 2, 4, N], BF16) as ylr, \
                    nc.sbuf_tensor(un("yca"), [128, 2, 4, N], BF16) as yca, \
                    nc.sbuf_tensor(un("gt"), [128, 2, 24, N], BF16) as gt, \
                    nc.sbuf_tensor(un("tmp"), [128, 2, 3, N], F32) as tmp_, \
                    nc.sbuf_tensor(un("mgb"), [128, KC, N], BF16) as mgb, \
                    nc.sbuf_tensor(un("yy"), [128, KC, N], F32) as yy, \
                    nc.sbuf_tensor(un("hs"), [128, KC, N], BF16) as hs, \
                    nc.sbuf_tensor(un("rs"), [128, 2, N], F32) as rs:
                vb = load_vec(l)
                stage = xx[:].rearrange("p a c n -> p (a c n)").rearrange("p (s w) -> p s w", s=4)
                sbs = k.bufs(4, "stg")
                load_weight(wna[:], w_bna[l], stage, sbs)
                load_weight(wlr[:], w_blru[l], stage, sbs)
                load_weight(wca[:], w_bca[l], stage, sbs)
                load_weight(wo[:], w_out[l], stage, sbs)
                k.barrier()
                xb = k.bufs(2, "x"); ynb = k.bufs(2, "yna"); ylb = k.bufs(2, "ylr"); ycb = k.bufs(2, "yca"); gtb = k.bufs(2, "gt")
                tb_ = [k.bufs(3, "tmpa"), k.bufs(3, "tmpb")]; mgbb = k.bufs(KC, "mg"); yb = k.buf("y"); hsb = k.buf("hs"); rsb = k.buf("rs")
                pb = k.bufs(8, "psb")
                it = 0
                for s in range(NS):
                    for ti in range(NT):
                        t0 = ti * N
                        sl = it % 2
                        it += 1
                        xv = xx[:, sl]
                        k.dma(SP, xv, xT_tile(s, t0, N), xb[sl], writes=[xb[sl]])
                        k.dma(SP, yna[:, sl], ynaT[s].rearrange("(c p) t -> p c t", p=128)[:, :, t0:t0 + N], ynb[sl], writes=[ynb[sl]])
                        k.dma(SP, ylr[:, sl], ylruT[s].rearrange("(c p) t -> p c t", p=128)[:, :, t0:t0 + N], ylb[sl], writes=[ylb[sl]])
                        k.dma(SP, yca[:, sl], ycaT[s].rearrange("(c p) t -> p c t", p=128)[:, :, t0:t0 + N], ycb[sl], writes=[ycb[sl]])
                        for q in range(4):
                            k.dma(SP, gt[:, sl, 6 * q:6 * q + 6, :], gtT[s].rearrange("(c p) t -> p c t", p=128)[:, 6 * q:6 * q + 6, t0:t0 + N],
                                  gtb[sl], writes=[gtb[sl]])
                        for oc in range(KC):
                            ba = oc % 2
                            tmp = tmp_[:, ba]
                            tb = tb_[ba]
                            pa = ps[:, ba, :]
                            pc = ps[:, 2 + ba, 0:N]
                            for h in range(4):
                                k.op(PE, lambda e, h=h, oc=oc: e.matmul(pa[:, 0:N], lhsT=wna[:, h, oc * 128:(oc + 1) * 128], rhs=yna[:, sl, h, :],
                                                                    start=(h == 0), stop=(h == 3)), reads=[ynb[sl]], writes=[pb[ba]])
                            for c in range(4):
                                k.op(PE, lambda e, c=c, oc=oc: e.matmul(pa[:, N:2 * N], lhsT=wlr[:, c, oc * 128:(oc + 1) * 128], rhs=ylr[:, sl, c, :],
                                                                    start=(c == 0), stop=(c == 3)), reads=[ylb[sl]], writes=[pb[ba]])
                            for c in range(4):
                                k.op(PE, lambda e, c=c, oc=oc: e.matmul(pc, lhsT=wca[:, c, oc * 128:(oc + 1) * 128], rhs=yca[:, sl, c, :],
                                                                    start=(c == 0), stop=(c == 3)), reads=[ycb[sl]], writes=[pb[2 + ba]])
                            k.op(DVE, lambda e, oc=oc: e.tensor_tensor(out=tmp[:, 0, :], in0=pa[:, 0:N], in1=gt[:, sl, oc, :], op=ALU.mult),
                                 reads=[pb[ba], gtb[sl]], writes=[tb[0]])
                            k.op(DVE, lambda e, oc=oc: e.tensor_tensor(out=tmp[:, 1, :], in0=pa[:, N:2 * N], in1=gt[:, sl, 8 + oc, :], op=ALU.mult),
                                 reads=[pb[ba], gtb[sl]], writes=[tb[1]])
                            k.op(DVE, lambda e, oc=oc: e.tensor_tensor(out=tmp[:, 2, :], in0=pc, in1=gt[:, sl, 16 + oc, :], op=ALU.mult),
                                 reads=[pb[2 + ba], gtb[sl]], writes=[tb[2]])
                            k.op(POOL, lambda e: e.tensor_tensor(out=tmp[:, 0, :], in0=tmp[:, 0, :], in1=tmp[:, 1, :], op=ALU.add),
                                 reads=[tb[0], tb[1]], writes=[tb[0]])
                            k.op(POOL, lambda e, oc=oc: e.tensor_tensor(out=mgb[:, oc, :], in0=tmp[:, 0, :], in1=tmp[:, 2, :], op=ALU.add),
                                 reads=[tb[0], tb[2]], writes=[mgbb[oc]])
                        for oc in range(KC):
                            pd = ps[:, 5 + oc % 2, 0:N]
                            for c in range(KC):
                                k.op(PE, lambda e, c=c, oc=oc: e.matmul(pd, lhsT=wo[:, c, oc * 128:(oc + 1) * 128], rhs=mgb[:, c, :],
                                                                    start=(c == 0), stop=(c == KC - 1)), reads=[mgbb[c]], writes=[pb[5 + oc % 2]])
                            k.op(ACT, lambda e, oc=oc: e.activation(out=yy[:, oc, :], in_=pd, func=AF.Copy), reads=[pb[5 + oc % 2]], writes=[yb])
                        postnorm_residual(yy[:], yb, xv, xb[sl], hs[:], hsb, vec[:, V_GMPOST:V_GMPOST + 8], ps[:, 4, 0:N], pb[4],
                                          rs[:, 0, :], rsb, False)
                        k.dma(POOL, xT_tile(s, t0, N), xv, xb[sl], reads=[xb[sl]])
                k.barrier()

        build.phases = dict(ffn=phase_ffn, mixin=phase_mixin, na=phase_na, lru=phase_lru, merge=phase_merge)
        if only == 'ffn1':
            phase_ffn(0, w_up1, w_dn1, V_G1PRE, V_G1POST, first=True, last=True)
        if only is not None and only.startswith('upto'):
            n = int(only[4:])
            plist = [lambda: phase_ffn(0, w_up1, w_dn1, V_G1PRE, V_G1POST, first=True, last=False), lambda: phase_mixin(0),
                     lambda: phase_na(0), lambda: phase_lru(0), lambda: phase_merge(0),
                     lambda: phase_ffn(0, w_up2, w_dn2, V_G2PRE, V_G2POST, first=False, last=True)]
            for f in plist[:n]:
                f()
        for l in range(L if only is None else 0):
            phase_ffn(l, w_up1, w_dn1, V_G1PRE, V_G1POST, first=(l == 0), last=False)
            phase_mixin(l)
            phase_na(l)
            phase_lru(l)
            phase_merge(l)
            phase_ffn(l, w_up2, w_dn2, V_G2PRE, V_G2POST, first=False, last=(l == L - 1))
    return nc


def _cols(v, nch):
    return np.ascontiguousarray(np.asarray(v, np.float32).reshape(nch, 128).T)


def host_prep(inputs, L):
    vec = np.zeros((L, 128, NVEC), np.float32)
    for l in range(L):
        for off, name in ((V_G1PRE, 'g_ffn1_pre'), (V_G1POST, 'g_ffn1_post'), (V_GMPRE, 'g_mix_pre'), (V_GMPOST, 'g_mix_post'),
                          (V_G2PRE, 'g_ffn2_pre'), (V_G2POST, 'g_ffn2_post'), (V_GMEM, 'g_mem')):
            vec[l, :, off:off + 8] = _cols(inputs[name][l], 8)
        vec[l, :, V_BG:V_BG + 24] = _cols(inputs['b_gate'][l], 24)
        for i in range(4):
            vec[l, :, V_CW + 4 * i:V_CW + 4 * i + 4] = _cols(inputs['conv_w'][l, i], 4)
        vec[l, :, V_CB:V_CB + 4] = _cols(inputs['conv_b'][l], 4)
        for d in range(2):
            vec[l, :, V_BA + 4 * d:V_BA + 4 * d + 4] = _cols(inputs['lru_ba'][l, d], 4)
            vec[l, :, V_BI + 4 * d:V_BI + 4 * d + 4] = _cols(inputs['lru_bi'][l, d], 4)
            vec[l, :, V_LAM + 4 * d:V_LAM + 4 * d + 4] = _cols(inputs['lru_lambda'][l, d], 4)
    kr2 = np.arange(2)[:, None, None, None]
    kc = np.arange(64)[None, :, None, None]
    sl = np.arange(16)[None, None, :, None]
    qc = np.arange(64)[None, None, None, :]
    dr = kr2 + 14 - sl + 0 * kc + 0 * qc
    dc = kc - qc + 15 + 0 * kr2 + 0 * sl
    ws = np.clip(qc - 8, 0, 48)
    colok = (kc >= ws) & (kc < ws + 16)
    ok = (dr >= 0) & (dr <= 14) & colok
    rpb = np.asarray(inputs['na_rpb'], np.float32)[:L]
    g = rpb[:, :, np.clip(dr, 0, 14), np.clip(dc, 0, 30)]
    rpbg = np.ascontiguousarray(g.reshape(L, H_NA, 128, 1024))
    cmask = np.ascontiguousarray(np.where(ok, 0.0, NEG).astype(np.float32).reshape(128, 1024))
    return vec, rpbg, cmask


WNAMES = ['w_ffn1_up', 'w_ffn1_down', 'w_ffn2_up', 'w_ffn2_down', 'w_in', 'w_gate', 'w_mem_kv',
          'w_branch_na', 'w_branch_lru', 'w_branch_ca', 'w_out', 'lru_wa', 'lru_wi']


def run(inputs, xs, mems, T, L, NS, ncores, only=None):
    vec, rpbg, cmask = host_prep(inputs, L)
    nc = build(T=T, L=L, NS=NS, only=only)
    common = {n: np.ascontiguousarray(np.asarray(inputs[n], np.float32)[:L]) for n in WNAMES}
    common.update(vecs=vec, rpbg=rpbg, cmask=cmask, cident=np.eye(128, dtype=np.float32))
    in_maps = []
    for c in range(ncores):
        m = dict(common)
        m['x0'] = np.ascontiguousarray(xs[c], dtype=np.float32)
        m['mem0'] = np.ascontiguousarray(mems[c], dtype=np.float32)
        in_maps.append(m)
    res = run_bass_kernel_spmd(nc, in_maps, core_ids=list(range(ncores)))
    return [r['yout'] for r in res.results]


def kernel(**inputs):
    xp = np.asarray(inputs['x_prompt'], np.float32)
    xs_ = np.asarray(inputs['x_sample'], np.float32)
    mp = np.asarray(inputs['mem_prompt'], np.float32)
    ms = np.asarray(inputs['mem_sample'], np.float32)
    xs = [np.stack([xp[c], xs_[c % 2]]) for c in range(8)]
    mems = [np.stack([mp[c], ms[c % 2]]) for c in range(8)]
    outs = run(inputs, xs, mems, T=8192, L=4, NS=2, ncores=8)
    y_prompt = np.stack([outs[c][0] for c in range(8)]).astype(np.float32)
    y_sample = np.stack([outs[c][1] for c in range(2)]).astype(np.float32)
    return (y_prompt, y_sample)
```
